# Optimizing a Trainium2 kernel written in Bass

```python
import math
import jax, jax.numpy as jnp
from jax import lax
import numpy as np

D_MODEL = 2048
BATCH = 1
SEQ = 16384
DEPTH = 2

CHUNK = 64
SSM_WIDTH = D_MODEL // 2
SSM_GROUP = 16
SSM_GROUPS = SSM_WIDTH // SSM_GROUP
SSM_STATE = 64
SSM_DT_MIN = 0.001
SSM_DT_MAX = 0.1
GDN_HEADS = 8
GDN_DK = 128
GDN_DV = 128
GDN_KDIM = GDN_HEADS * GDN_DK
GDN_VDIM = GDN_HEADS * GDN_DV
GDN_CONV_CH = 2 * GDN_KDIM + GDN_VDIM
CONV_WIDTH = 4
GDN_DT_MIN = 0.001
GDN_DT_MAX = 0.1
GDN_A_MAX = 16.0
FFN_HIDDEN = -(-8 * D_MODEL // (3 * 256)) * 256
IN_SIZES = (SSM_WIDTH, GDN_KDIM, GDN_KDIM, GDN_VDIM, GDN_VDIM, GDN_HEADS, GDN_HEADS, D_MODEL, D_MODEL)
PROJ_IN = sum(IN_SIZES)
DEEPNORM_ALPHA = (2 * DEPTH) ** 0.25
DEEPNORM_BETA = (8 * DEPTH) ** -0.25
LN_EPS = 1e-5
NORM_EPS = 1e-6

kernel_name = 'hybrid_s5_gdn_deepnorm_adaln'


def layer_norm(x, gain=None, bias=None):
    xf = x.astype(jnp.float32)
    mu = xf.mean(-1, keepdims=True)
    var = jnp.square(xf - mu).mean(-1, keepdims=True)
    y = (xf - mu) * lax.rsqrt(var + LN_EPS)
    if gain is not None:
        y = y * gain.astype(jnp.float32) + bias.astype(jnp.float32)
    return y.astype(x.dtype)


def causal_dwconv(x, w):
    k = w.shape[0]
    xp = jnp.pad(x, ((0, 0), (k - 1, 0), (0, 0)))
    return lax.conv_general_dilated(xp, w.astype(x.dtype)[:, None, :], window_strides=(1,), padding='VALID',
                                    dimension_numbers=('NWC', 'WIO', 'NWC'), feature_group_count=x.shape[-1])


def s5_mixer(u, lam_re, lam_im, log_dt, b_re, b_im, c_re, c_im, d_skip, w_glu, b_glu):
    f32 = jnp.float32
    bsz, seq, _ = u.shape
    uf = u.astype(f32).reshape(bsz, seq, SSM_GROUPS, SSM_GROUP)
    lr, li = lam_re.astype(f32), lam_im.astype(f32)
    dt = jnp.exp(log_dt.astype(f32))[:, None]
    mag = jnp.exp(lr * dt)
    ang = li * dt
    ab_re, ab_im = mag * jnp.cos(ang), mag * jnp.sin(ang)
    den = lr * lr + li * li
    nr, ni = ab_re - 1.0, ab_im
    f_re = (nr * lr + ni * li) / den
    f_im = (ni * lr - nr * li) / den
    br, bi = b_re.astype(f32), b_im.astype(f32)
    bb_re = f_re[..., None] * br - f_im[..., None] * bi
    bb_im = f_re[..., None] * bi + f_im[..., None] * br
    bu_re = jnp.einsum('blgh,gph->blgp', uf, bb_re)
    bu_im = jnp.einsum('blgh,gph->blgp', uf, bb_im)
    a_re = jnp.broadcast_to(ab_re, bu_re.shape)
    a_im = jnp.broadcast_to(ab_im, bu_im.shape)

    def combine(e1, e2):
        a1r, a1i, b1r, b1i = e1
        a2r, a2i, b2r, b2i = e2
        return (a2r * a1r - a2i * a1i, a2r * a1i + a2i * a1r,
                a2r * b1r - a2i * b1i + b2r, a2r * b1i + a2i * b1r + b2i)

    _, _, xs_re, xs_im = lax.associative_scan(combine, (a_re, a_im, bu_re, bu_im), axis=1)
    y = (jnp.einsum('blgp,ghp->blgh', xs_re, c_re.astype(f32))
         - jnp.einsum('blgp,ghp->blgh', xs_im, c_im.astype(f32)))
    y = y.reshape(bsz, seq, SSM_WIDTH) + d_skip.astype(f32) * uf.reshape(bsz, seq, SSM_WIDTH)
    z = jax.nn.gelu(y)
    y = z * jax.nn.sigmoid(z @ w_glu.astype(f32) + b_glu.astype(f32))
    return y.astype(u.dtype)


def chunk_gated_delta_rule(q, k, v, beta, g):
    bsz, seq, nh, dk = q.shape
    dv = v.shape[-1]
    nc = seq // CHUNK

    def to_chunks(t):
        return t.reshape(bsz, nc, CHUNK, nh, -1).transpose(0, 3, 1, 2, 4)

    q, k, v = to_chunks(q), to_chunks(k), to_chunks(v)
    beta = beta.reshape(bsz, nc, CHUNK, nh).transpose(0, 3, 1, 2)
    g_cum = jnp.cumsum(g.reshape(bsz, nc, CHUNK, nh).transpose(0, 3, 1, 2), axis=-1)
    causal = jnp.tril(jnp.ones((CHUNK, CHUNK), dtype=bool))
    strict = jnp.tril(jnp.ones((CHUNK, CHUNK), dtype=bool), k=-1)
    diff = g_cum[..., :, None] - g_cum[..., None, :]
    decay = jnp.exp(jnp.where(causal, diff, -jnp.inf))
    k_beta = k * beta[..., None]
    m = jnp.where(strict, jnp.einsum('bhnid,bhnjd->bhnij', k_beta, k) * decay, 0.0)
    eye = jnp.eye(CHUNK, dtype=jnp.float32)
    t_inv = lax.linalg.triangular_solve(eye + m, jnp.broadcast_to(eye, m.shape), left_side=True,
                                        lower=True, unit_diagonal=True)
    u = jnp.einsum('bhnij,bhnjd->bhnid', t_inv, v * beta[..., None])
    w = jnp.einsum('bhnij,bhnjd->bhnid', t_inv, k_beta * jnp.exp(g_cum)[..., None])
    attn = jnp.einsum('bhnid,bhnjd->bhnij', q, k) * decay
    q_dec = q * jnp.exp(g_cum)[..., None]
    k_tail = k * jnp.exp(g_cum[..., -1:] - g_cum)[..., None]
    g_last = jnp.exp(g_cum[..., -1])

    def step(state, xs):
        u_c, w_c, attn_c, qd_c, kt_c, gl_c = xs
        v_new = u_c - jnp.einsum('bhcd,bhde->bhce', w_c, state)
        o_c = jnp.einsum('bhcd,bhde->bhce', qd_c, state) + jnp.einsum('bhij,bhje->bhie', attn_c, v_new)
        state = state * gl_c[..., None, None] + jnp.einsum('bhcd,bhce->bhde', kt_c, v_new)
        return state, o_c

    xs = tuple(jnp.moveaxis(t, 2, 0) for t in (u, w, attn, q_dec, k_tail, g_last))
    s0 = jnp.zeros((bsz, nh, dk, dv), jnp.float32)
    _, o = lax.scan(step, s0, xs)
    return o.transpose(1, 0, 3, 2, 4).reshape(bsz, seq, nh, dv)


def gated_deltanet(q, k, v, z, beta_logit, a_logit, conv_w, a_log, dt_bias, norm_w):
    f32 = jnp.float32
    bsz, seq, _ = q.shape
    qkv = jax.nn.silu(causal_dwconv(jnp.concatenate([q, k, v], axis=-1), conv_w)).astype(f32)
    q, k, v = jnp.split(qkv, [GDN_KDIM, 2 * GDN_KDIM], axis=-1)
    q = q.reshape(bsz, seq, GDN_HEADS, GDN_DK)
    k = k.reshape(bsz, seq, GDN_HEADS, GDN_DK)
    v = v.reshape(bsz, seq, GDN_HEADS, GDN_DV)
    q = q * lax.rsqrt(jnp.sum(q * q, -1, keepdims=True) + NORM_EPS) * (GDN_DK ** -0.5)
    k = k * lax.rsqrt(jnp.sum(k * k, -1, keepdims=True) + NORM_EPS)
    beta = jax.nn.sigmoid(beta_logit.astype(f32))
    g = -jnp.exp(a_log.astype(f32)) * jax.nn.softplus(a_logit.astype(f32) + dt_bias.astype(f32))
    o = chunk_gated_delta_rule(q, k, v, beta, g)
    o = o * lax.rsqrt(jnp.mean(o * o, -1, keepdims=True) + NORM_EPS) * norm_w.astype(f32)
    o = o * jax.nn.silu(z.astype(f32).reshape(bsz, seq, GDN_HEADS, GDN_DV))
    return o.reshape(bsz, seq, GDN_VDIM).astype(z.dtype)


def hybrid_mixer(h, w_in, lam_re, lam_im, log_dt, b_re, b_im, c_re, c_im, d_skip, w_glu, b_glu,
                 conv_w, a_log, dt_bias, norm_w, w_up_ssm, w_up_gdn, w_out):
    splits = np.cumsum(IN_SIZES)[:-1].tolist()
    u, q, k, v, z, beta_l, a_l, gate_s, gate_g = jnp.split(h @ w_in, splits, axis=-1)
    y_s = s5_mixer(u, lam_re, lam_im, log_dt, b_re, b_im, c_re, c_im, d_skip, w_glu, b_glu) @ w_up_ssm
    y_g = gated_deltanet(q, k, v, z, beta_l, a_l, conv_w, a_log, dt_bias, norm_w) @ w_up_gdn
    merged = jax.nn.sigmoid(gate_s) * y_s + jax.nn.sigmoid(gate_g) * y_g
    return merged @ w_out


def swiglu(h, w_in, w_out):
    gate, up = jnp.split(h @ w_in, 2, axis=-1)
    return (jax.nn.silu(gate) * up) @ w_out


def setup_inputs(seed: int = 0) -> dict:
    key = jax.random.key(seed)
    keys = iter(jax.random.split(key, 40))
    f32 = jnp.float32
    nl = DEPTH

    def normal(shape, scale):
        return scale * jax.random.normal(next(keys), shape, f32)

    def uniform(shape, lo, hi):
        return jax.random.uniform(next(keys), shape, f32, lo, hi)

    x = normal((BATCH, SEQ, D_MODEL), 1.0)
    c = normal((BATCH, D_MODEL), 1.0)
    w_ada = normal((nl, D_MODEL, 6 * D_MODEL), D_MODEL ** -0.5)
    b_ada = normal((nl, 6 * D_MODEL), 0.02)
    w_in = normal((nl, D_MODEL, PROJ_IN), D_MODEL ** -0.5)
    n_idx = jnp.arange(SSM_STATE, dtype=f32)
    ssm_lam_re = -0.5 + normal((nl, SSM_GROUPS, SSM_STATE), 0.01)
    ssm_lam_im = math.pi * n_idx + normal((nl, SSM_GROUPS, SSM_STATE), 0.01)
    ssm_log_dt = uniform((nl, SSM_GROUPS), math.log(SSM_DT_MIN), math.log(SSM_DT_MAX))
    ssm_b_re = normal((nl, SSM_GROUPS, SSM_STATE, SSM_GROUP), (2 * SSM_GROUP) ** -0.5)
    ssm_b_im = normal((nl, SSM_GROUPS, SSM_STATE, SSM_GROUP), (2 * SSM_GROUP) ** -0.5)
    ssm_c_re = normal((nl, SSM_GROUPS, SSM_GROUP, SSM_STATE), SSM_STATE ** -0.5)
    ssm_c_im = normal((nl, SSM_GROUPS, SSM_GROUP, SSM_STATE), SSM_STATE ** -0.5)
    ssm_d = normal((nl, SSM_WIDTH), 1.0)
    ssm_w_glu = normal((nl, SSM_WIDTH, SSM_WIDTH), SSM_WIDTH ** -0.5)
    ssm_b_glu = normal((nl, SSM_WIDTH), 0.02)
    gdn_conv_w = normal((nl, CONV_WIDTH, GDN_CONV_CH), CONV_WIDTH ** -0.5)
    gdn_a_log = jnp.log(uniform((nl, GDN_HEADS), 1.0, GDN_A_MAX))
    dt = jnp.exp(uniform((nl, GDN_HEADS), math.log(GDN_DT_MIN), math.log(GDN_DT_MAX)))
    gdn_dt_bias = dt + jnp.log(-jnp.expm1(-dt))
    gdn_norm_w = 1.0 + normal((nl, GDN_DV), 0.02)
    w_up_ssm = normal((nl, SSM_WIDTH, D_MODEL), SSM_WIDTH ** -0.5)
    w_up_gdn = normal((nl, GDN_VDIM, D_MODEL), GDN_VDIM ** -0.5)
    w_mix_out = normal((nl, D_MODEL, D_MODEL), DEEPNORM_BETA * D_MODEL ** -0.5)
    ln1_g = 1.0 + normal((nl, D_MODEL), 0.02)
    ln1_b = normal((nl, D_MODEL), 0.02)
    ffn_w_in = normal((nl, D_MODEL, 2 * FFN_HIDDEN), D_MODEL ** -0.5)
    ffn_w_out = normal((nl, FFN_HIDDEN, D_MODEL), DEEPNORM_BETA * FFN_HIDDEN ** -0.5)
    ln2_g = 1.0 + normal((nl, D_MODEL), 0.02)
    ln2_b = normal((nl, D_MODEL), 0.02)
    return {'x': x, 'c': c, 'w_ada': w_ada, 'b_ada': b_ada, 'w_in': w_in,
            'ssm_lam_re': ssm_lam_re, 'ssm_lam_im': ssm_lam_im, 'ssm_log_dt': ssm_log_dt,
            'ssm_b_re': ssm_b_re, 'ssm_b_im': ssm_b_im, 'ssm_c_re': ssm_c_re, 'ssm_c_im': ssm_c_im,
            'ssm_d': ssm_d, 'ssm_w_glu': ssm_w_glu, 'ssm_b_glu': ssm_b_glu,
            'gdn_conv_w': gdn_conv_w, 'gdn_a_log': gdn_a_log, 'gdn_dt_bias': gdn_dt_bias, 'gdn_norm_w': gdn_norm_w,
            'w_up_ssm': w_up_ssm, 'w_up_gdn': w_up_gdn, 'w_mix_out': w_mix_out,
            'ln1_g': ln1_g, 'ln1_b': ln1_b, 'ffn_w_in': ffn_w_in, 'ffn_w_out': ffn_w_out,
            'ln2_g': ln2_g, 'ln2_b': ln2_b}


def reference(x, c, w_ada, b_ada, w_in, ssm_lam_re, ssm_lam_im, ssm_log_dt, ssm_b_re, ssm_b_im,
              ssm_c_re, ssm_c_im, ssm_d, ssm_w_glu, ssm_b_glu, gdn_conv_w, gdn_a_log, gdn_dt_bias,
              gdn_norm_w, w_up_ssm, w_up_gdn, w_mix_out, ln1_g, ln1_b, ffn_w_in, ffn_w_out, ln2_g, ln2_b):
    for l in range(DEPTH):
        mod = (jax.nn.silu(c) @ w_ada[l] + b_ada[l])[:, None, :]
        sh_m, sc_m, g_m, sh_f, sc_f, g_f = jnp.split(mod, 6, axis=-1)
        h = layer_norm(x) * (1 + sc_m) + sh_m
        y = hybrid_mixer(h, w_in[l], ssm_lam_re[l], ssm_lam_im[l], ssm_log_dt[l], ssm_b_re[l], ssm_b_im[l],
                         ssm_c_re[l], ssm_c_im[l], ssm_d[l], ssm_w_glu[l], ssm_b_glu[l],
                         gdn_conv_w[l], gdn_a_log[l], gdn_dt_bias[l], gdn_norm_w[l],
                         w_up_ssm[l], w_up_gdn[l], w_mix_out[l])
        x = layer_norm(DEEPNORM_ALPHA * x + g_m * y, ln1_g[l], ln1_b[l])
        h = layer_norm(x) * (1 + sc_f) + sh_f
        x = layer_norm(DEEPNORM_ALPHA * x + g_f * swiglu(h, ffn_w_in[l], ffn_w_out[l]), ln2_g[l], ln2_b[l])
    return x
```

```python
import os
import numpy as np
from contextlib import ExitStack
import concourse.bass as bass
import concourse.mybir as mybir
from concourse.bass_utils import run_bass_kernel_spmd

F32 = mybir.dt.float32
BF16 = mybir.dt.bfloat16
ALU = mybir.AluOpType
AF = mybir.ActivationFunctionType
AX = mybir.AxisListType

D = 2048
NCORES = 8
DEPTH = 2
PIN = 9232
FH = 5632
ALPHA = (2 * DEPTH) ** 0.25
SEM_LIMIT = 30000
DMA_K = 6


class FW:
    def __init__(self, nc):
        self.nc = nc
        self.eng = {"pe": nc.tensor, "act": nc.scalar, "dve": nc.vector, "pool": nc.gpsimd, "sp": nc.sync}
        self.root = ExitStack()
        self.semh = {}
        self.cur = {}
        self.cnt = {}
        self.gen = {}
        self.known = {e: {} for e in self.eng}
        self.bufs = {}
        for e in self.eng:
            self.gen[e] = 0
            self._newsem(e)
        self.dq = {}
        for q in ("sp", "act", "pool"):
            sems = []
            for i in range(DMA_K):
                key = ("d", q, i)
                self.semh[key] = self.root.enter_context(nc.semaphore(f"d_{q}_{i}"))
                sems.append(key)
            self.dq[q] = {"n": 0, "sems": sems, "hist": []}
        self.ccsem = ("cc",)
        self.semh[self.ccsem] = self.root.enter_context(nc.semaphore("ccsem"))
        self.ccn = 0
        self.ninst = 0

    def _newsem(self, e):
        key = ("e", e, self.gen[e])
        self.semh[key] = self.root.enter_context(self.nc.semaphore(f"s_{e}_{self.gen[e]}"))
        self.cur[e] = key
        self.cnt[e] = 0
        self.gen[e] += 1

    def _waits(self, e, r, w):
        need = {}

        def add(t):
            if t is None:
                return
            if need.get(t[0], 0) < t[1]:
                need[t[0]] = t[1]

        for k in r:
            b = self.bufs.get(k)
            if b:
                add(b[0])
        for k in w:
            b = self.bufs.get(k)
            if b:
                add(b[0])
                for t in b[1]:
                    add(t)
        eng = self.eng[e]
        kn = self.known[e]
        for sk, v in need.items():
            if kn.get(sk, 0) < v:
                eng.wait_ge(self.semh[sk], v)
                kn[sk] = v

    def _record(self, ticket, r, w):
        for k in w:
            self.bufs[k] = [ticket, []]
        for k in r:
            b = self.bufs.get(k)
            if b is None:
                self.bufs[k] = [None, [ticket]]
            else:
                b[1].append(ticket)

    def op(self, e, fn, r=(), w=()):
        self._waits(e, r, w)
        ins = fn(self.eng[e])
        self.cnt[e] += 1
        ins.then_inc(self.semh[self.cur[e]], 1)
        ticket = (self.cur[e], self.cnt[e])
        self.ninst += 1
        self._record(ticket, r, w)
        if self.cnt[e] >= SEM_LIMIT:
            self._newsem(e)
        return ticket

    def dma(self, q, out, in_, r=(), w=()):
        self._waits(q, r, w)
        st = self.dq[q]
        n = st["n"]
        i = n % DMA_K
        sk = st["sems"][i]
        tgt = 16 * (n // DMA_K + 1)
        eng = self.eng[q]
        if n >= DMA_K:
            prev = 16 * (n // DMA_K)
            if self.known[q].get(sk, 0) < prev:
                eng.wait_ge(self.semh[sk], prev)
                self.known[q][sk] = prev
        ins = eng.dma_start(out=out, in_=in_)
        ins.then_inc(self.semh[sk], 16)
        st["n"] = n + 1
        ticket = (sk, tgt)
        self.ninst += 1
        self._record(ticket, r, w)
        return ticket

    def allgather(self, out, in_, r=(), w=(), ncores=NCORES):
        if ncores == 1:
            return self.dma("pool", out, in_, r=r, w=w)
        self._waits("pool", r, w)
        ins = self.nc.gpsimd.collective_compute(
            "AllGather", ALU.bypass, replica_groups=[list(range(ncores))], ins=[in_], outs=[out]
        )
        self.ccn += 1
        ins.then_inc(self.semh[self.ccsem])
        ticket = (self.ccsem, self.ccn)
        self._record(ticket, r, w)
        return ticket

    def barrier(self):
        tickets = []
        for e in self.eng:
            if self.cnt[e] > 0:
                tickets.append((self.cur[e], self.cnt[e]))
        for q, st in self.dq.items():
            n = st["n"]
            for j in range(max(0, n - DMA_K), n):
                tickets.append((st["sems"][j % DMA_K], 16 * (j // DMA_K + 1)))
        if self.ccn:
            tickets.append((self.ccsem, self.ccn))
        for e in self.eng:
            kn = self.known[e]
            for sk, v in tickets:
                if kn.get(sk, 0) < v:
                    self.eng[e].wait_ge(self.semh[sk], v)
                    kn[sk] = v
        self.bufs = {}


class Prog:
    def __init__(self, T, ncores=NCORES, depth=DEPTH, debug=(), stop_after=None):
        self.T = T
        self.NT = T // 128
        self.N8 = T // 8
        self.HT = T + 3
        self.depth = depth
        self.ncores = ncores
        self.debug = set(debug)
        self.stop_after = stop_after
        self.nc = bass.Bass("TRN2", target_bir_lowering=False)
        self.fw = FW(self.nc)
        self.inputs = {}
        self.uid = 0

    def din(self, name, shape, dt=F32):
        t = self.nc.dram_tensor(name, list(shape), dt, kind="ExternalInput")
        self.inputs[name] = (tuple(shape), dt)
        return t.ap()

    def dscr(self, name, shape, dt=F32, shared=False):
        kind = "ExternalOutput" if name in self.debug else "Internal"
        if shared:
            return self.nc.dram_tensor(name, list(shape), dt, kind="Internal", addr_space="Shared").ap()
        return self.nc.dram_tensor(name, list(shape), dt, kind=kind).ap()

    def sb(self, st, name, shape, dt=F32):
        cache = st.__dict__.setdefault("_tiles", {})
        if name in cache:
            return cache[name]
        self.uid += 1
        t = st.enter_context(self.nc.sbuf_tensor(f"{name}_{self.uid}", list(shape), dt))
        cache[name] = t
        return t

    def ps(self, st, name, shape, dt=F32):
        self.uid += 1
        return st.enter_context(self.nc.psum_tensor(f"{name}_{self.uid}", list(shape), dt))


def _mm_group(pe, out, pairs):
    ins = None
    n = len(pairs)
    for i, (l, r) in enumerate(pairs):
        ins = pe.matmul(out, l, r, start=(i == 0), stop=(i == n - 1))
    return ins


def phase_mod(P, c_pk, w_ada, b_ada, MODB):
    nc, fw = P.nc, P.fw
    with ExitStack() as st:
        cs32 = P.sb(st, "cs32", [128, 16])
        csil = P.sb(st, "csil", [128, 16])
        csb = P.sb(st, "csb", [128, 16, 128], BF16)
        wts = [P.sb(st, f"wada{i}", [128, 16, 512], BF16) for i in range(3)]
        bts = [P.sb(st, f"bada{i}", [128, 512]) for i in range(2)]
        outs = [P.sb(st, f"mout{i}", [128, 512]) for i in range(2)]
        pss = [P.ps(st, f"mps{i}", [128, 512]) for i in range(2)]
        fw.dma("sp", cs32[:], c_pk, w=["cs32"])
        fw.op("act", lambda e: e.activation(out=csil[:], in_=cs32[:], func=AF.Silu), r=["cs32"], w=["csil"])
        fw.op("dve", lambda e: e.tensor_copy(out=csb[:], in_=csil[:].unsqueeze(2).to_broadcast([128, 16, 128])),
              r=["csil"], w=["csb"])
        i = 0
        for l in range(P.depth):
            for cb in range(24):
                wt = wts[i % 3]
                bt = bts[i % 2]
                ot = outs[i % 2]
                pt = pss[i % 2]
                c0 = cb * 512
                fw.dma("pool", wt[:], w_ada[l, :, c0:c0 + 512].rearrange("(kt p) n -> p kt n", p=128),
                       w=[("wada", i % 3)])
                fw.dma("sp", bt[:], b_ada[l:l + 1, c0:c0 + 512].partition_broadcast(128), w=[("bada", i % 2)])
                fw.op("pe", lambda e, wt=wt, pt=pt: _mm_group(e, pt[:], [(csb[:, kt, :], wt[:, kt, :]) for kt in range(16)]),
                      r=["csb", ("wada", i % 3)], w=[("mps", i % 2)])
                fw.op("dve", lambda e, ot=ot, pt=pt, bt=bt: e.tensor_tensor(out=ot[:], in0=pt[:], in1=bt[:], op=ALU.add),
                      r=[("mps", i % 2), ("bada", i % 2)], w=[("mout", i % 2)])
                fw.dma("sp", MODB[l][:, c0:c0 + 512], ot[:], r=[("mout", i % 2)], w=[("MODB", l, cb)])
                i += 1
        fw.barrier()


def load_mod(P, st, MODB, l, idx, name, plus1=False):
    fw = P.fw
    t = P.sb(st, name, [128, D])
    fw.dma("sp", t[:], MODB[l][:, idx * D:(idx + 1) * D], r=[("MODB", l, c) for c in range(idx * 4, idx * 4 + 4)], w=[name])
    if plus1:
        fw.op("pool", lambda e: e.tensor_scalar(out=t[:], in0=t[:], scalar1=1.0, scalar2=None, op0=ALU.add), r=[name], w=[name])
    return t


def rsqrt_eps(fw, out, in_, eps, r, w, scale=1.0):
    fw.op("act", lambda e: e.activation(out=out, in_=in_, func=AF.Sqrt, bias=eps, scale=scale), r=r, w=w)
    fw.op("dve", lambda e: e.reciprocal(out=out, in_=out), r=w, w=w)


def ln_stats(P, fw, xt, xkey, stats, mv, rstd, nmr, key):
    def f(e):
        ins = None
        for j in range(4):
            ins = e.bn_stats(out=stats[:, j, :], in_=xt[:, j * 512:(j + 1) * 512])
        return ins
    fw.op("dve", f, r=[xkey], w=[key + "st"])
    fw.op("dve", lambda e: e.bn_aggr(out=mv[:], in_=stats[:].rearrange("p a b -> p (a b)")), r=[key + "st"], w=[key + "mv"])
    rsqrt_eps(fw, rstd[:], mv[:, 1:2], 1e-5, [key + "mv"], [key + "rs"])
    fw.op("dve", lambda e: e.scalar_tensor_tensor(out=nmr[:], in0=mv[:, 0:1], scalar=-1.0, in1=rstd[:], op0=ALU.mult, op1=ALU.mult),
          r=[key + "mv", key + "rs"], w=[key + "nm"])


def phase_ln_mod_T(P, st_out, srcs, MODB, l, isc, ish, hT, hkey, ident_bf):
    nc, fw = P.nc, P.fw
    with ExitStack() as st:
        sc1 = load_mod(P, st, MODB, l, isc, "sc1", plus1=True)
        sh = load_mod(P, st, MODB, l, ish, "sh")
        xts = [P.sb(st, f"xt{i}", [128, D]) for i in range(2)]
        xns = [P.sb(st, f"xn{i}", [128, D]) for i in range(2)]
        hbs = [P.sb(st, f"hb{i}", [128, D], BF16) for i in range(2)]
        stats = [P.sb(st, f"stt{i}", [128, 4, 6]) for i in range(2)]
        mvs = [P.sb(st, f"mv{i}", [128, 2]) for i in range(2)]
        rss = [P.sb(st, f"rs{i}", [128, 1]) for i in range(2)]
        nms = [P.sb(st, f"nm{i}", [128, 1]) for i in range(2)]
        pts = [P.ps(st, f"tps{i}", [128, 16, 128], BF16) for i in range(2)]
        for i, (src, rkeys, col0, nrows) in enumerate(srcs):
            b = i % 2
            xt, xn, hb, pt = xts[b], xns[b], hbs[b], pts[b]
            k = f"p1_{b}"
            if nrows < 128:
                fw.op("pool", lambda e, xt=xt: e.memset(xt[:], 0.0), w=[k + "x"])
            fw.dma("sp", xt[0:nrows, :], src, r=rkeys, w=[k + "x"])
            ln_stats(P, fw, xt, k + "x", stats[b], mvs[b], rss[b], nms[b], k)
            fw.op("act", lambda e, xt=xt, xn=xn, b=b: e.activation(out=xn[:], in_=xt[:], func=AF.Identity, bias=nms[b][:], scale=rss[b][:]),
                  r=[k + "x", k + "rs", k + "nm"], w=[k + "xn"])
            fw.op("pool", lambda e, xn=xn: e.tensor_tensor(out=xn[:], in0=xn[:], in1=sc1[:], op=ALU.mult), r=[k + "xn", "sc1"], w=[k + "xn"])
            fw.op("dve", lambda e, xn=xn, hb=hb: e.tensor_tensor(out=hb[:], in0=xn[:], in1=sh[:], op=ALU.add), r=[k + "xn", "sh"], w=[k + "hb"])

            def tr(e, hb=hb, pt=pt):
                ins = None
                for ft in range(16):
                    ins = e.transpose(out=pt[:, ft, :], in_=hb[:, ft * 128:(ft + 1) * 128], identity=ident_bf[:])
                return ins
            fw.op("pe", tr, r=[k + "hb", "ident_bf"], w=[k + "tp"])
            eng = "act" if i % 2 == 0 else "dve"
            if eng == "act":
                fw.op("act", lambda e, pt=pt, col0=col0, nrows=nrows: e.copy(out=hT[:, :, col0:col0 + nrows], in_=pt[:, :, 0:nrows]),
                      r=[k + "tp"], w=[(hkey, col0)])
            else:
                fw.op("dve", lambda e, pt=pt, col0=col0, nrows=nrows: e.tensor_copy(out=hT[:, :, col0:col0 + nrows], in_=pt[:, :, 0:nrows]),
                      r=[k + "tp"], w=[(hkey, col0)])
        fw.barrier()


def hkeys(P, hkey, halo=True):
    ks = [(hkey, 3 + tt * 128) for tt in range(P.NT)]
    if halo:
        ks.append((hkey, 0))
    return ks


def phase_inproj(P, l, hT, w_in, cw_pk, nf, ident_f, QKV, UT, Z, BA, G):
    nc, fw = P.nc, P.fw
    T, NT, HT, N8 = P.T, P.NT, P.HT, P.N8
    with ExitStack() as st:
        wbs = [P.sb(st, f"wb{i}", [128, 16, 512], BF16) for i in range(3)]
        cw = P.sb(st, "cw", [128, 24, 4])
        fw.dma("sp", cw[:], cw_pk[l], w=["cw"])
        pres = [P.sb(st, f"pre{i}", [128, HT]) for i in range(2)]
        accs = [P.sb(st, f"acc{i}", [128, T]) for i in range(2)]
        stg = [P.sb(st, f"stg{i}", [128, 512]) for i in range(3)]
        pss = [P.ps(st, f"ips{i}", [128, 512]) for i in range(4)]
        allh = hkeys(P, "hT")
        wi = [0]
        pi = [0]
        si = [0]

        def load_w(c0, ncol):
            b = wi[0] % 3
            wi[0] += 1
            fw.dma("pool", wbs[b][:, :, 0:ncol], w_in[l, :, c0:c0 + ncol].rearrange("(kt p) n -> p kt n", p=128), w=[("wb", b)])
            return wbs[b], ("wb", b)

        def nextps():
            b = pi[0] % 4
            pi[0] += 1
            return pss[b], ("ips", b)

        def colblocks(c_lo, c_hi):
            out = []
            c = c_lo
            while c < c_hi:
                n = min(512, c_hi - c)
                out.append((c, n))
                c += n
            return out

        for g4 in range(6):
            wb, wk = load_w(1024 + g4 * 512, 512)
            for j in range(4):
                ct = g4 * 4 + j
                pre = pres[ct % 2]
                acc = accs[ct % 2]
                pk, ak = ("pre", ct % 2), ("acc", ct % 2)
                for bi, (c0, n) in enumerate([(0, 3)] + colblocks(3, HT)):
                    pt, ptk = nextps()
                    fw.op("pe", lambda e, pt=pt, wb=wb, j=j, c0=c0, n=n: _mm_group(
                        e, pt[:, 0:n], [(wb[:, kt, j * 128:(j + 1) * 128], hT[:, kt, c0:c0 + n]) for kt in range(16)]),
                        r=[wk] + allh, w=[ptk])
                    if c0 == 0:
                        fw.op("dve", lambda e, pt=pt, pre=pre: e.tensor_scalar(out=pre[:, 0:3], in0=pt[:, 0:3], scalar1=nf[:, 0:1], scalar2=None, op0=ALU.mult),
                              r=[ptk, "nf"], w=[pk])
                    else:
                        fw.op("act", lambda e, pt=pt, pre=pre, c0=c0, n=n: e.copy(out=pre[:, c0:c0 + n], in_=pt[:, 0:n]), r=[ptk], w=[pk])
                fw.op("dve", lambda e, pre=pre, acc=acc, ct=ct: e.tensor_scalar(out=acc[:], in0=pre[:, 0:T], scalar1=cw[:, ct, 0:1], scalar2=None, op0=ALU.mult),
                      r=[pk, "cw"], w=[ak])
                for k in range(1, 4):
                    fw.op("dve", lambda e, pre=pre, acc=acc, ct=ct, k=k: e.scalar_tensor_tensor(
                        out=acc[:], in0=pre[:, k:k + T], scalar=cw[:, ct, k:k + 1], in1=acc[:], op0=ALU.mult, op1=ALU.add),
                        r=[pk, "cw", ak], w=[ak])
                fw.op("act", lambda e, acc=acc: e.activation(out=acc[:], in_=acc[:], func=AF.Silu), r=[ak], w=[ak])
                for t4 in range(NT // 4):
                    pt, ptk = nextps()

                    def tr(e, pt=pt, acc=acc, t4=t4):
                        ins = None
                        for q in range(4):
                            tt = t4 * 4 + q
                            ins = e.transpose(out=pt[:, q * 128:(q + 1) * 128], in_=acc[:, tt * 128:(tt + 1) * 128], identity=ident_f[:])
                        return ins
                    fw.op("pe", tr, r=[ak, "ident_f"], w=[ptk])
                    sb_ = si[0] % 3
                    si[0] += 1
                    sg = stg[sb_]
                    if t4 % 2 == 0:
                        fw.op("dve", lambda e, sg=sg, pt=pt: e.tensor_copy(out=sg[:], in_=pt[:]), r=[ptk], w=[("stg", sb_)])
                    else:
                        fw.op("act", lambda e, sg=sg, pt=pt: e.copy(out=sg[:], in_=pt[:]), r=[ptk], w=[("stg", sb_)])
                    fw.dma("sp", QKV[t4 * 512:(t4 + 1) * 512, ct * 128:(ct + 1) * 128].rearrange("(q p) c -> p q c", p=128),
                           sg[:].rearrange("p (q c) -> p q c", q=4), r=[("stg", sb_)], w=[("QKV", t4, ct)])
        ust = [P.sb(st, f"ust{i}", [128, 8, N8]) for i in range(2)]
        for g4 in range(2):
            wb, wk = load_w(g4 * 512, 512)
            for j in range(4):
                ct = g4 * 4 + j
                us = ust[ct % 2]
                uk = ("ust", ct % 2)
                for bi, (c0, n) in enumerate(colblocks(3, HT)):
                    pt, ptk = nextps()
                    fw.op("pe", lambda e, pt=pt, wb=wb, j=j, c0=c0, n=n: _mm_group(
                        e, pt[:, 0:n], [(wb[:, kt, j * 128:(j + 1) * 128], hT[:, kt, c0:c0 + n]) for kt in range(16)]),
                        r=[wk] + allh, w=[ptk])
                    n0 = (c0 - 3) // 8
                    en = "act" if bi % 2 == 0 else "dve"
                    if en == "act":
                        fw.op("act", lambda e, pt=pt, us=us, n0=n0, n=n: e.copy(out=us[:, :, n0:n0 + n // 8], in_=pt[:, 0:n].rearrange("p (n t) -> p t n", t=8)),
                              r=[ptk], w=[uk])
                    else:
                        fw.op("dve", lambda e, pt=pt, us=us, n0=n0, n=n: e.tensor_copy(out=us[:, :, n0:n0 + n // 8], in_=pt[:, 0:n].rearrange("p (n t) -> p t n", t=8)),
                              r=[ptk], w=[uk])
                fw.dma("sp", UT[ct * 128:(ct + 1) * 128, :, :], us[:], r=[uk], w=[("UT", ct)])
        tm = [(4096 + i * 512, 512, Z, i * 512, False) for i in range(2)]
        tm.append((5120, 16, BA, 0, False))
        tm += [(5136 + i * 512, 512, G, i * 512, True) for i in range(8)]
        for (c0, ncol, dst, d0, sig) in tm:
            wb, wk = load_w(c0, ncol)
            for tt in range(NT):
                pt, ptk = nextps()
                fw.op("pe", lambda e, pt=pt, wb=wb, tt=tt, ncol=ncol: _mm_group(
                    e, pt[:, 0:ncol], [(hT[:, kt, 3 + tt * 128:3 + (tt + 1) * 128], wb[:, kt, 0:ncol]) for kt in range(16)]),
                    r=[wk, ("hT", 3 + tt * 128)], w=[ptk])
                sb_ = si[0] % 3
                si[0] += 1
                sg = stg[sb_]
                if sig:
                    fw.op("act", lambda e, sg=sg, pt=pt, ncol=ncol: e.activation(out=sg[:, 0:ncol], in_=pt[:, 0:ncol], func=AF.Sigmoid), r=[ptk], w=[("stg", sb_)])
                elif tt % 2 == 0:
                    fw.op("dve", lambda e, sg=sg, pt=pt, ncol=ncol: e.tensor_copy(out=sg[:, 0:ncol], in_=pt[:, 0:ncol]), r=[ptk], w=[("stg", sb_)])
                else:
                    fw.op("act", lambda e, sg=sg, pt=pt, ncol=ncol: e.copy(out=sg[:, 0:ncol], in_=pt[:, 0:ncol]), r=[ptk], w=[("stg", sb_)])
                fw.dma("sp", dst[tt * 128:(tt + 1) * 128, d0:d0 + ncol], sg[:, 0:ncol], r=[("stg", sb_)], w=[(id(dst), tt, d0)])
        fw.barrier()


TWO_PI = 6.283185307179586


def _sincos(P, fw, st, ang, akey, n, tag):
    I32 = mybir.dt.int32
    outs = []
    for which, shift in (("s", 0.0), ("c", 1.5707963267948966)):
        y = P.sb(st, f"sc_y{tag}{which}", [128, n])
        ki = P.sb(st, f"sc_k{tag}{which}", [128, n], I32)
        kf = P.sb(st, f"sc_f{tag}{which}", [128, n])
        yk, kk, fk = f"scy{tag}{which}", f"sck{tag}{which}", f"scf{tag}{which}"
        fw.op("dve", lambda e, y=y, shift=shift: e.tensor_scalar(out=y[:], in0=ang, scalar1=shift, scalar2=1.0 / TWO_PI, op0=ALU.add, op1=ALU.mult), r=[akey], w=[yk])
        fw.op("dve", lambda e, y=y, ki=ki: e.tensor_copy(out=ki[:], in_=y[:]), r=[yk], w=[kk])
        fw.op("dve", lambda e, kf=kf, ki=ki: e.tensor_copy(out=kf[:], in_=ki[:]), r=[kk], w=[fk])
        fw.op("dve", lambda e, y=y, kf=kf: e.tensor_tensor(out=y[:], in0=y[:], in1=kf[:], op=ALU.subtract), r=[yk, fk], w=[yk])
        fw.op("dve", lambda e, y=y: e.tensor_scalar(out=y[:], in0=y[:], scalar1=TWO_PI, scalar2=3.1415925, op0=ALU.mult, op1=ALU.min), r=[yk], w=[yk])
        fw.op("dve", lambda e, y=y: e.tensor_scalar(out=y[:], in0=y[:], scalar1=-3.1415925, scalar2=None, op0=ALU.max), r=[yk], w=[yk])
        fw.op("act", lambda e, y=y: e.activation(out=y[:], in_=y[:], func=AF.Sin), r=[yk], w=[yk])
        outs.append((y, yk))
    return outs[0], outs[1]


def _cpow(P, fw, st, lr, li, ld, keys, n, kmode, kval, tag):
    dtb = P.sb(st, f"cp_dt{tag}", [128, n])
    kl = P.sb(st, f"cp_kl{tag}", [128, n])
    ka = P.sb(st, f"cp_ka{tag}", [128, n])
    dk, klk, kak = f"cpdt{tag}", f"cpkl{tag}", f"cpka{tag}"
    fw.op("act", lambda e: e.activation(out=dtb[:], in_=ld, func=AF.Exp), r=keys, w=[dk])
    fw.op("dve", lambda e: e.tensor_tensor(out=kl[:], in0=lr, in1=dtb[:], op=ALU.mult), r=keys + [dk], w=[klk])
    fw.op("dve", lambda e: e.tensor_tensor(out=ka[:], in0=li, in1=dtb[:], op=ALU.mult), r=keys + [dk], w=[kak])
    if kmode == "const":
        if kval != 1.0:
            fw.op("dve", lambda e: e.tensor_scalar(out=kl[:], in0=kl[:], scalar1=float(kval), scalar2=None, op0=ALU.mult), r=[klk], w=[klk])
            fw.op("dve", lambda e: e.tensor_scalar(out=ka[:], in0=ka[:], scalar1=float(kval), scalar2=None, op0=ALU.mult), r=[kak], w=[kak])
    elif kmode == "ps":
        fw.op("dve", lambda e: e.tensor_scalar(out=kl[:], in0=kl[:], scalar1=kval[0], scalar2=None, op0=ALU.mult), r=[klk, kval[1]], w=[klk])
        fw.op("dve", lambda e: e.tensor_scalar(out=ka[:], in0=ka[:], scalar1=kval[0], scalar2=None, op0=ALU.mult), r=[kak, kval[1]], w=[kak])
    else:
        fw.op("dve", lambda e: e.tensor_tensor(out=kl[:], in0=kl[:], in1=kval[0], op=ALU.mult), r=[klk, kval[1]], w=[klk])
        fw.op("dve", lambda e: e.tensor_tensor(out=ka[:], in0=ka[:], in1=kval[0], op=ALU.mult), r=[kak, kval[1]], w=[kak])
    fw.op("act", lambda e: e.activation(out=kl[:], in_=kl[:], func=AF.Exp), r=[klk], w=[klk])
    (s, sk), (c, ck) = _sincos(P, fw, st, ka[:], kak, n, tag)
    fw.op("pool", lambda e: e.tensor_tensor(out=s[:], in0=s[:], in1=kl[:], op=ALU.mult), r=[sk, klk], w=[sk])
    fw.op("pool", lambda e: e.tensor_tensor(out=c[:], in0=c[:], in1=kl[:], op=ALU.mult), r=[ck, klk], w=[ck])
    return (c, ck), (s, sk)


def _cmul(fw, eng, outr, outi, ar, ai, br, bi, tmp, rkeys, wkr, wki, tk, neg_im=False):
    fw.op(eng, lambda e: e.tensor_tensor(out=outr, in0=ar, in1=br, op=ALU.mult), r=rkeys, w=[wkr])
    fw.op(eng, lambda e: e.tensor_tensor(out=tmp, in0=ai, in1=bi, op=ALU.mult), r=rkeys, w=[tk])
    fw.op(eng, lambda e: e.tensor_tensor(out=outr, in0=outr, in1=tmp, op=ALU.subtract), r=[wkr, tk], w=[wkr])
    fw.op(eng, lambda e: e.tensor_tensor(out=outi, in0=ar, in1=bi, op=ALU.mult), r=rkeys, w=[wki])
    fw.op(eng, lambda e: e.tensor_tensor(out=tmp, in0=ai, in1=br, op=ALU.mult), r=rkeys, w=[tk])
    fw.op(eng, lambda e: e.tensor_tensor(out=outi, in0=outi, in1=tmp, op=ALU.add), r=[wki, tk], w=[wki])
    if neg_im:
        fw.op(eng, lambda e: e.tensor_scalar(out=outi, in0=outi, scalar1=-1.0, scalar2=None, op0=ALU.mult), r=[wki], w=[wki])


def s5_precompute(P, st_w, l, s5in, ident_f):
    nc, fw = P.nc, P.fw
    MBre = P.sb(st_w, "MBre", [128, 64, 64], BF16)
    MBim = P.sb(st_w, "MBim", [128, 64, 64], BF16)
    TP = P.sb(st_w, "TP", [128, 64, 128], BF16)
    MCre = P.sb(st_w, "MCre", [128, 32, 128], BF16)
    MCim = P.sb(st_w, "MCim", [128, 32, 128], BF16)
    abr = P.sb(st_w, "abr", [128, 32])
    abi = P.sb(st_w, "abi", [128, 32])
    with ExitStack() as st:
        kA = P.sb(st, "kA", [128, 1])
        kC = P.sb(st, "kC", [128, 4, 128])
        kF = P.sb(st, "kF", [128, 4, 128])
        cmask = P.sb(st, "cmask", [128, 128])
        fw.dma("sp", kA[:], s5in["kA"], w=["kA"])
        fw.dma("sp", kC[:], s5in["kC"], w=["kC"])
        fw.dma("sp", kF[:], s5in["kF"], w=["kF"])
        fw.dma("sp", cmask[:], s5in["cmask"], w=["cmask"])
        lrC = P.sb(st, "lrC", [128, 32]); liC = P.sb(st, "liC", [128, 32]); ldC = P.sb(st, "ldC", [128, 32])
        fw.dma("sp", lrC[:], s5in["lrC"][l], w=["lrC"]); fw.dma("sp", liC[:], s5in["liC"][l], w=["liC"]); fw.dma("sp", ldC[:], s5in["ldC"][l], w=["ldC"])
        (c8, c8k), (s8, s8k) = _cpow(P, fw, st, lrC[:], liC[:], ldC[:], ["lrC", "liC", "ldC"], 32, "const", 8.0, "C")
        fw.op("dve", lambda e: e.tensor_copy(out=abr[:], in_=c8[:]), r=[c8k], w=["abr"])
        fw.op("dve", lambda e: e.tensor_copy(out=abi[:], in_=s8[:]), r=[s8k], w=["abi"])
        n = 512
        STOP = int(os.environ.get("S5_STOP", "99"))
        A = {k: P.sb(st, "A_" + k, [128, 8, 64]) for k in ("lr", "li", "ld", "bre", "bim")}
        B = {k: P.sb(st, "B_" + k, [128, 4, 128]) for k in ("lr", "li", "ld", "cre", "cim")}
        tmp = P.sb(st, "s5tmp", [128, n]); tmp2 = P.sb(st, "s5tmp2", [128, n])
        fre = P.sb(st, "fre", [128, n]); fim = P.sb(st, "fim", [128, n])
        bbr = P.sb(st, "bbr", [128, n]); bbi = P.sb(st, "bbi", [128, n])
        Er = P.sb(st, "Er", [128, n]); Ei = P.sb(st, "Ei", [128, n])
        Fr = P.sb(st, "Fr", [128, n]); Fi = P.sb(st, "Fi", [128, n])
        Mr = P.sb(st, "Mr", [128, n]); Mi = P.sb(st, "Mi", [128, n])
        ErT = P.sb(st, "ErT", [128, 128]); EiT = P.sb(st, "EiT", [128, 128])
        pT = P.ps(st, "s5pT", [128, 4, 128])
        pTPs = [P.ps(st, f"s5pTP{i}", [128, 512]) for i in range(2)]
        fl = lambda t: t[:].rearrange("p a b -> p (a b)")
        for o in range(int(os.environ.get("S5_NOCT", "8")) if STOP > 1 else 0):
            akeys = []
            for k, nm in (("lr", "lrA"), ("li", "liA"), ("ld", "ldA"), ("bre", "bReA"), ("bim", "bImA")):
                fw.dma("sp", A[k][:], s5in[nm][l][:, o * 8:(o + 1) * 8, :], w=["A_" + k]); akeys.append("A_" + k)
            bkeys = []
            for k, nm in (("lr", "lrB"), ("li", "liB"), ("ld", "ldB"), ("cre", "cReB"), ("cim", "cImB")):
                fw.dma("sp", B[k][:], s5in[nm][l][:, o * 4:(o + 1) * 4, :], w=["B_" + k]); bkeys.append("B_" + k)
            lk = ["A_lr", "A_li", "A_ld"]
            (c1, c1k), (s1, s1k) = _cpow(P, fw, st, fl(A["lr"]), fl(A["li"]), fl(A["ld"]), lk, n, "const", 1.0, "A1")
            fw.op("dve", lambda e: e.tensor_scalar(out=c1[:], in0=c1[:], scalar1=-1.0, scalar2=None, op0=ALU.add), r=[c1k], w=[c1k])
            fw.op("pool", lambda e: e.tensor_tensor(out=tmp[:], in0=fl(A["lr"]), in1=fl(A["lr"]), op=ALU.mult), r=lk, w=["s5tmp"])
            fw.op("pool", lambda e: e.tensor_tensor(out=tmp2[:], in0=fl(A["li"]), in1=fl(A["li"]), op=ALU.mult), r=lk, w=["s5tmp2"])
            fw.op("pool", lambda e: e.tensor_tensor(out=tmp[:], in0=tmp[:], in1=tmp2[:], op=ALU.add), r=["s5tmp", "s5tmp2"], w=["s5tmp"])
            fw.op("dve", lambda e: e.reciprocal(out=tmp[:], in_=tmp[:]), r=["s5tmp"], w=["s5tmp"])
            fw.op("dve", lambda e: e.tensor_tensor(out=fre[:], in0=c1[:], in1=fl(A["lr"]), op=ALU.mult), r=[c1k] + lk, w=["fre"])
            fw.op("dve", lambda e: e.tensor_tensor(out=tmp2[:], in0=s1[:], in1=fl(A["li"]), op=ALU.mult), r=[s1k] + lk, w=["s5tmp2"])
            fw.op("dve", lambda e: e.tensor_tensor(out=fre[:], in0=fre[:], in1=tmp2[:], op=ALU.add), r=["fre", "s5tmp2"], w=["fre"])
            fw.op("dve", lambda e: e.tensor_tensor(out=fre[:], in0=fre[:], in1=tmp[:], op=ALU.mult), r=["fre", "s5tmp"], w=["fre"])
            fw.op("dve", lambda e: e.tensor_tensor(out=fim[:], in0=s1[:], in1=fl(A["lr"]), op=ALU.mult), r=[s1k] + lk, w=["fim"])
            fw.op("dve", lambda e: e.tensor_tensor(out=tmp2[:], in0=c1[:], in1=fl(A["li"]), op=ALU.mult), r=[c1k] + lk, w=["s5tmp2"])
            fw.op("dve", lambda e: e.tensor_tensor(out=fim[:], in0=fim[:], in1=tmp2[:], op=ALU.subtract), r=["fim", "s5tmp2"], w=["fim"])
            fw.op("dve", lambda e: e.tensor_tensor(out=fim[:], in0=fim[:], in1=tmp[:], op=ALU.mult), r=["fim", "s5tmp"], w=["fim"])
            if STOP <= 2:
                break
            _cmul(fw, "pool", bbr[:], bbi[:], fre[:], fim[:], fl(A["bre"]), fl(A["bim"]), tmp2[:], ["fre", "fim", "A_bre", "A_bim"], "bbr", "bbi", "s5tmp2")
            (ck_, ckk), (sk_, skk) = _cpow(P, fw, st, fl(A["lr"]), fl(A["li"]), fl(A["ld"]), lk, n, "ps", (kA[:, 0:1], "kA"), "Ak")
            _cmul(fw, "dve", Er[:], Ei[:], ck_[:], sk_[:], bbr[:], bbi[:], tmp2[:], [ckk, skk, "bbr", "bbi"], "Er", "Ei", "s5tmp2")
            fw.op("act", lambda e, o=o: e.copy(out=MBre[:, o * 8:(o + 1) * 8, :].rearrange("p a b -> p (a b)"), in_=Er[:]), r=["Er"], w=[("MB", o)])
            fw.op("act", lambda e, o=o: e.copy(out=MBim[:, o * 8:(o + 1) * 8, :].rearrange("p a b -> p (a b)"), in_=Ei[:]), r=["Ei"], w=[("MBi", o)])
            if STOP <= 3:
                break
            bl = ["B_lr", "B_li", "B_ld"]
            (cf, cfk), (sf, sfk) = _cpow(P, fw, st, fl(B["lr"]), fl(B["li"]), fl(B["ld"]), bl, n, "t", (fl(kF), "kF"), "BF")
            _cmul(fw, "pool", Fr[:], Fi[:], fl(B["cre"]), fl(B["cim"]), cf[:], sf[:], tmp2[:], ["B_cre", "B_cim", cfk, sfk], "Fr", "Fi", "s5tmp2", neg_im=True)
            (cm, cmk), (sm, smk) = _cpow(P, fw, st, fl(B["lr"]), fl(B["li"]), fl(B["ld"]), bl, n, "t", (fl(kC), "kC"), "BC")
            _cmul(fw, "dve", Mr[:], Mi[:], fl(B["cre"]), fl(B["cim"]), cm[:], sm[:], tmp[:], ["B_cre", "B_cim", cmk, smk], "Mr", "Mi", "s5tmp", neg_im=True)
            fw.op("act", lambda e, o=o: e.copy(out=MCre[:, o * 4:(o + 1) * 4, :].rearrange("p a b -> p (a b)"), in_=Mr[:]), r=["Mr"], w=[("MC", o)])
            fw.op("act", lambda e, o=o: e.copy(out=MCim[:, o * 4:(o + 1) * 4, :].rearrange("p a b -> p (a b)"), in_=Mi[:]), r=["Mi"], w=[("MCi", o)])
            if STOP <= 4:
                break
            for j in range(4):
                def trf(e, j=j):
                    e.transpose(out=pT[:, 0, :], in_=Er[:, j * 128:(j + 1) * 128], identity=ident_f[:])
                    return e.transpose(out=pT[:, 1, :], in_=Ei[:, j * 128:(j + 1) * 128], identity=ident_f[:])
                fw.op("pe", trf, r=["Er", "Ei", "ident_f"], w=["s5pT"])
                fw.op("dve", lambda e: e.tensor_copy(out=ErT[:], in_=pT[:, 0, :]), r=["s5pT"], w=["ErT"])
                fw.op("dve", lambda e: e.tensor_copy(out=EiT[:], in_=pT[:, 1, :]), r=["s5pT"], w=["EiT"])
                for gp in range(2):
                    sl = slice(gp * 64, (gp + 1) * 64)
                    pq = pTPs[gp]
                    fw.op("pe", lambda e, pq=pq, sl=sl, j=j: _mm_group(e, pq[:, 0:128], [
                        (ErT[sl, :], Fr[sl, j * 128:(j + 1) * 128]), (EiT[sl, :], Fi[sl, j * 128:(j + 1) * 128])]),
                        r=["ErT", "EiT", "Fr", "Fi"], w=[("s5pTP", gp)])
                    g = o * 8 + 2 * j + gp
                    fw.op("dve", lambda e, pq=pq, g=g: e.tensor_tensor(out=TP[:, g, :], in0=pq[:, 0:128], in1=cmask[:], op=ALU.mult),
                          r=[("s5pTP", gp), "cmask"], w=[("TP", g)])
        fw.barrier()
    return dict(MBre=MBre, MBim=MBim, TP=TP, MCre=MCre, MCim=MCim, abr=abr, abi=abi)


def phase_s5_main(P, l, W, UT, YT, S5B, S5G, mask):
    nc, fw = P.nc, P.fw
    T, N8 = P.T, P.N8
    MBre, MBim, TP, MCre, MCim, abr, abi = (W[k] for k in ("MBre", "MBim", "TP", "MCre", "MCim", "abr", "abi"))
    wkeys = [("MB", o) for o in range(8)] + [("MBi", o) for o in range(8)]
    with ExitStack() as st:
        Lre = P.sb(st, "Lre", [128, 32, N8]); Lim = P.sb(st, "Lim", [128, 32, N8])
        Uos = [P.sb(st, f"Uo{i}", [128, 8, N8], BF16) for i in range(2)]
        pss = [P.ps(st, f"s5ps{i}", [128, 512]) for i in range(4)]
        xin_r = P.sb(st, "xin_r", [128, 32]); xin_i = P.sb(st, "xin_i", [128, 32])
        pi = 0
        for o in range(8):
            Uo = Uos[o % 2]; uk = ("Uo", o % 2)
            fw.dma("pool", Uo[:], UT[o * 128:(o + 1) * 128].rearrange("(g h) t n -> (h t) g n", g=8), r=[("UT", o)], w=[uk])
            for j in range(4):
                pr = 4 * o + j
                for (MB, L, lk) in ((MBre, Lre, "Lre"), (MBim, Lim, "Lim")):
                    ps = pss[pi % 4]; pk = ("s5ps", pi % 4); pi += 1

                    def mm(e, ps=ps, MB=MB, Uo=Uo, j=j, pr=pr):
                        e.matmul(ps[0:64, 0:N8], MB[:, 2 * pr, :], Uo[:, 2 * j, :], start=True, stop=True)
                        return e.matmul(ps[64:128, 0:N8], MB[:, 2 * pr + 1, :], Uo[:, 2 * j + 1, :], start=True, stop=True)
                    fw.op("pe", mm, r=[uk] + wkeys, w=[pk])
                    if lk == "Lre":
                        fw.op("act", lambda e, ps=ps, L=L, pr=pr: e.copy(out=L[:, pr, :], in_=ps[:, 0:N8]), r=[pk], w=[(lk, pr)])
                    else:
                        fw.op("dve", lambda e, ps=ps, L=L, pr=pr: e.tensor_copy(out=L[:, pr, :], in_=ps[:, 0:N8]), r=[pk], w=[(lk, pr)])
        fw.barrier()
        with ExitStack() as st2:
            Ar_ = P.sb(st2, "redAr", [128, 32, N8 // 2]); Ai_ = P.sb(st2, "redAi", [128, 32, N8 // 2])
            Br_ = P.sb(st2, "redBr", [128, 32, max(N8 // 4, 1)]); Bi_ = P.sb(st2, "redBi", [128, 32, max(N8 // 4, 1)])
            tmp = P.sb(st2, "redT", [128, 32, N8 // 2])
            cr = P.sb(st2, "redcr", [128, 32]); ci = P.sb(st2, "redci", [128, 32]); c2 = P.sb(st2, "redc2", [128, 32]); c3 = P.sb(st2, "redc3", [128, 32])
            fw.op("dve", lambda e: e.tensor_copy(out=cr[:], in_=abr[:]), r=["abr"], w=["red"])
            fw.op("dve", lambda e: e.tensor_copy(out=ci[:], in_=abi[:]), r=["abi", "red"], w=["red"])
            src = (Lre, Lim); n = N8; lvl = 0
            bufs = [(Ar_, Ai_), (Br_, Bi_)]
            while n > 1:
                h = n // 2
                dr, di = bufs[lvl % 2]
                sr, si = src
                Er = sr[:, :, 0:n].rearrange("p a (m two) -> p a m two", two=2)[:, :, :, 0]
                Or = sr[:, :, 0:n].rearrange("p a (m two) -> p a m two", two=2)[:, :, :, 1]
                Ei = si[:, :, 0:n].rearrange("p a (m two) -> p a m two", two=2)[:, :, :, 0]
                Oi = si[:, :, 0:n].rearrange("p a (m two) -> p a m two", two=2)[:, :, :, 1]
                crb = cr[:].unsqueeze(2).to_broadcast([128, 32, h]); cib = ci[:].unsqueeze(2).to_broadcast([128, 32, h])
                DR, DI, TM = dr[:, :, 0:h], di[:, :, 0:h], tmp[:, :, 0:h]
                K = ["red"]
                fw.op("dve", lambda e, DR=DR, Er=Er, crb=crb: e.tensor_tensor(out=DR, in0=Er, in1=crb, op=ALU.mult), r=K, w=K)
                fw.op("dve", lambda e, DR=DR, Or=Or: e.tensor_tensor(out=DR, in0=DR, in1=Or, op=ALU.add), r=K, w=K)
                fw.op("dve", lambda e, TM=TM, Ei=Ei, cib=cib: e.tensor_tensor(out=TM, in0=Ei, in1=cib, op=ALU.mult), r=K, w=K)
                fw.op("dve", lambda e, DR=DR, TM=TM: e.tensor_tensor(out=DR, in0=DR, in1=TM, op=ALU.subtract), r=K, w=K)
                fw.op("dve", lambda e, DI=DI, Ei=Ei, crb=crb: e.tensor_tensor(out=DI, in0=Ei, in1=crb, op=ALU.mult), r=K, w=K)
                fw.op("dve", lambda e, DI=DI, Oi=Oi: e.tensor_tensor(out=DI, in0=DI, in1=Oi, op=ALU.add), r=K, w=K)
                fw.op("dve", lambda e, TM=TM, Er=Er, cib=cib: e.tensor_tensor(out=TM, in0=Er, in1=cib, op=ALU.mult), r=K, w=K)
                fw.op("dve", lambda e, DI=DI, TM=TM: e.tensor_tensor(out=DI, in0=DI, in1=TM, op=ALU.add), r=K, w=K)
                fw.op("dve", lambda e: e.tensor_tensor(out=c2[:], in0=cr[:], in1=ci[:], op=ALU.mult), r=K, w=K)
                fw.op("dve", lambda e: e.tensor_tensor(out=c3[:], in0=ci[:], in1=ci[:], op=ALU.mult), r=K, w=K)
                fw.op("dve", lambda e: e.tensor_tensor(out=cr[:], in0=cr[:], in1=cr[:], op=ALU.mult), r=K, w=K)
                fw.op("dve", lambda e: e.tensor_tensor(out=cr[:], in0=cr[:], in1=c3[:], op=ALU.subtract), r=K, w=K)
                fw.op("dve", lambda e: e.tensor_scalar(out=ci[:], in0=c2[:], scalar1=2.0, scalar2=None, op0=ALU.mult), r=K, w=K)
                src = (dr, di); n = h; lvl += 1
            fr, fi = src
            bnc = P.sb(st2, "s5bnc", [128, 64])
            fw.op("dve", lambda e: e.tensor_copy(out=bnc[:, 0:32], in_=fr[:, :, 0]), r=["red"], w=["bnc"])
            fw.op("dve", lambda e: e.tensor_copy(out=bnc[:, 32:64], in_=fi[:, :, 0]), r=["red"], w=["bnc"])
            fw.dma("pool", S5B[:, :], bnc[:], r=["bnc"], w=["S5B"])
            fw.allgather(S5G.opt(), S5B.opt(), r=["S5B"], w=["S5G"], ncores=P.ncores)
            gat = P.sb(st2, "s5gat", [128, P.ncores, 64])
            fw.dma("pool", gat[:], S5G.rearrange("(r p) f -> p r f", p=128), r=["S5G"], w=["gat"])
            accr = P.sb(st2, "accr", [128, 32]); acci = P.sb(st2, "acci", [128, 32]); t1 = P.sb(st2, "cht1", [128, 32]); t2 = P.sb(st2, "cht2", [128, 32])
            K = ["chain"]
            fw.op("dve", lambda e: e.memset(xin_r[:], 0.0), w=["xin"])
            fw.op("dve", lambda e: e.memset(xin_i[:], 0.0), r=["xin"], w=["xin"])
            for r_ in range(P.ncores - 1):
                if r_ == 0:
                    fw.op("dve", lambda e: e.tensor_copy(out=accr[:], in_=gat[:, 0, 0:32]), r=["gat"], w=K)
                    fw.op("dve", lambda e: e.tensor_copy(out=acci[:], in_=gat[:, 0, 32:64]), r=["gat"] + K, w=K)
                else:
                    fw.op("dve", lambda e: e.tensor_tensor(out=t1[:], in0=accr[:], in1=cr[:], op=ALU.mult), r=K + ["red"], w=K)
                    fw.op("dve", lambda e: e.tensor_tensor(out=t2[:], in0=acci[:], in1=ci[:], op=ALU.mult), r=K, w=K)
                    fw.op("dve", lambda e: e.tensor_tensor(out=t1[:], in0=t1[:], in1=t2[:], op=ALU.subtract), r=K, w=K)
                    fw.op("dve", lambda e: e.tensor_tensor(out=t2[:], in0=accr[:], in1=ci[:], op=ALU.mult), r=K, w=K)
                    fw.op("dve", lambda e: e.tensor_tensor(out=acci[:], in0=acci[:], in1=cr[:], op=ALU.mult), r=K, w=K)
                    fw.op("dve", lambda e: e.tensor_tensor(out=acci[:], in0=acci[:], in1=t2[:], op=ALU.add), r=K, w=K)
                    fw.op("dve", lambda e, r_=r_: e.tensor_tensor(out=acci[:], in0=acci[:], in1=gat[:, r_, 32:64], op=ALU.add), r=K + ["gat"], w=K)
                    fw.op("dve", lambda e, r_=r_: e.tensor_tensor(out=accr[:], in0=t1[:], in1=gat[:, r_, 0:32], op=ALU.add), r=K + ["gat"], w=K)
                fw.op("dve", lambda e, r_=r_: e.scalar_tensor_tensor(out=xin_r[:], in0=accr[:], scalar=mask[:, r_ + 1:r_ + 2], in1=xin_r[:], op0=ALU.mult, op1=ALU.add), r=K + ["xin", "mask"], w=["xin"])
                fw.op("dve", lambda e, r_=r_: e.scalar_tensor_tensor(out=xin_i[:], in0=acci[:], scalar=mask[:, r_ + 1:r_ + 2], in1=xin_i[:], op0=ALU.mult, op1=ALU.add), r=K + ["xin", "mask"], w=["xin"])
            fw.barrier()
        with ExitStack() as st3:
            ta = P.sb(st3, "scta", [128, 32]); tb = P.sb(st3, "sctb", [128, 32]); nabi = P.sb(st3, "nabi", [128, 32])
            K = ["scan"]
            fw.op("dve", lambda e: e.tensor_scalar(out=nabi[:], in0=abi[:], scalar1=-1.0, scalar2=None, op0=ALU.mult), r=["abi"], w=K)

            def step(prev_r, prev_i, n):
                fw.op("dve", lambda e: e.tensor_tensor(out=ta[:], in0=prev_r, in1=abr[:], op=ALU.mult), r=K, w=K)
                fw.op("dve", lambda e: e.tensor_tensor(out=Lre[:, :, n], in0=Lre[:, :, n], in1=ta[:], op=ALU.add), r=K, w=K)
                fw.op("dve", lambda e: e.tensor_tensor(out=ta[:], in0=prev_i, in1=nabi[:], op=ALU.mult), r=K, w=K)
                fw.op("dve", lambda e: e.tensor_tensor(out=Lre[:, :, n], in0=Lre[:, :, n], in1=ta[:], op=ALU.add), r=K, w=K)
                fw.op("dve", lambda e: e.tensor_tensor(out=tb[:], in0=prev_i, in1=abr[:], op=ALU.mult), r=K, w=K)
                fw.op("dve", lambda e: e.tensor_tensor(out=Lim[:, :, n], in0=Lim[:, :, n], in1=tb[:], op=ALU.add), r=K, w=K)
                fw.op("dve", lambda e: e.tensor_tensor(out=tb[:], in0=prev_r, in1=abi[:], op=ALU.mult), r=K, w=K)
                fw.op("dve", lambda e: e.tensor_tensor(out=Lim[:, :, n], in0=Lim[:, :, n], in1=tb[:], op=ALU.add), r=K, w=K)
            xr0 = P.sb(st3, "xr0", [128, 32]); xi0 = P.sb(st3, "xi0", [128, 32])
            fw.op("dve", lambda e: e.tensor_copy(out=xr0[:], in_=xin_r[:]), r=["xin"], w=K)
            fw.op("dve", lambda e: e.tensor_copy(out=xi0[:], in_=xin_i[:]), r=["xin"] + K, w=K)
            step(xr0[:], xi0[:], 0)
            for n in range(1, N8):
                step(Lre[:, :, n - 1], Lim[:, :, n - 1], n)
            fw.barrier()
        with ExitStack() as st4:
            XPr = P.sb(st4, "XPr", [128, 32, N8], BF16); XPi = P.sb(st4, "XPi", [128, 32, N8], BF16)
            fw.op("act", lambda e: e.copy(out=XPr[:, :, 0], in_=xin_r[:]), w=["XP"])
            fw.op("act", lambda e: e.copy(out=XPi[:, :, 0], in_=xin_i[:]), r=["XP"], w=["XP"])
            fw.op("act", lambda e: e.copy(out=XPr[:, :, 1:N8], in_=Lre[:, :, 0:N8 - 1]), r=["XP"], w=["XP"])
            fw.op("dve", lambda e: e.tensor_copy(out=XPi[:, :, 1:N8], in_=Lim[:, :, 0:N8 - 1]), r=["XP"], w=["XP"])
            Yos = [P.sb(st4, f"Yo{i}", [128, 8, N8]) for i in range(2)]
            allw = [("TP", g) for g in range(64)] + [("MC", o) for o in range(8)] + [("MCi", o) for o in range(8)]
            for o in range(8):
                Uo = Uos[o % 2]; uk = ("Uo", o % 2)
                Yo = Yos[o % 2]; yk = ("Yo", o % 2)
                fw.dma("pool", Uo[:], UT[o * 128:(o + 1) * 128].rearrange("(g h) t n -> (h t) g n", g=8), w=[uk])
                for gl in range(8):
                    g = 8 * o + gl; pr = g // 2; sl = slice(64 * (g % 2), 64 * (g % 2) + 64)
                    ps = pss[pi % 4]; pk = ("s5ps", pi % 4); pi += 1
                    fw.op("pe", lambda e, ps=ps, g=g, gl=gl, pr=pr, sl=sl, Uo=Uo: _mm_group(e, ps[:, 0:N8], [
                        (TP[:, g, :], Uo[:, gl, :]), (MCre[sl, pr, :], XPr[sl, pr, :]), (MCim[sl, pr, :], XPi[sl, pr, :])]),
                        r=[uk, "XP"] + allw, w=[pk])
                    if gl % 2 == 0:
                        fw.op("act", lambda e, ps=ps, Yo=Yo, gl=gl: e.copy(out=Yo[:, gl, :], in_=ps[:, 0:N8]), r=[pk], w=[yk])
                    else:
                        fw.op("dve", lambda e, ps=ps, Yo=Yo, gl=gl: e.tensor_copy(out=Yo[:, gl, :], in_=ps[:, 0:N8]), r=[pk], w=[yk])
                fw.dma("sp", YT[o * 128:(o + 1) * 128].rearrange("(g h) t n -> (h t) g n", g=8), Yo[:], r=[yk], w=[("YT", o)])
            fw.barrier()


def phase_s5_post(P, l, st_keep, UT, YT, d_pk, wglu, bglu_pk):
    nc, fw = P.nc, P.fw
    T, N8 = P.T, P.N8
    s5oT = P.sb(st_keep, "s5oT", [128, 8, T], BF16)
    with ExitStack() as st:
        zT = P.sb(st, "zT", [128, 8, T], BF16)
        dd = P.sb(st, "s5d", [128, 8]); bg = P.sb(st, "s5bg", [128, 8])
        fw.dma("sp", dd[:], d_pk[l], w=["s5d"]); fw.dma("sp", bg[:], bglu_pk[l], w=["s5bg"])
        wg = P.sb(st, "wglu", [128, 8, 1024], BF16)
        fw.dma("pool", wg[:], wglu[l].rearrange("(kt p) n -> p kt n", p=128), w=["wglu"])
        yts = [P.sb(st, f"pyt{i}", [128, T]) for i in range(2)]
        uts = [P.sb(st, f"put{i}", [128, T]) for i in range(2)]
        t1s = [P.sb(st, f"pt1{i}", [128, T]) for i in range(2)]
        for ct in range(8):
            b = ct % 2
            yt, ut, t1 = yts[b], uts[b], t1s[b]
            ky, ku, kt1 = ("pyt", b), ("put", b), ("pt1", b)
            fw.dma("sp", yt[:], YT[ct * 128:(ct + 1) * 128].rearrange("c t n -> c (t n)"), r=[("YT", ct)], w=[ky])
            fw.dma("sp", ut[:], UT[ct * 128:(ct + 1) * 128].rearrange("c t n -> c (t n)"), r=[("UT", ct)], w=[ku])
            fw.op("dve", lambda e, yt=yt, ut=ut, ct=ct: e.scalar_tensor_tensor(out=yt[:], in0=ut[:], scalar=dd[:, ct:ct + 1], in1=yt[:], op0=ALU.mult, op1=ALU.add),
                  r=[ky, ku, "s5d"], w=[ky])
            fw.op("pool", lambda e, yt=yt, t1=t1: e.tensor_tensor(out=t1[:], in0=yt[:], in1=yt[:], op=ALU.mult), r=[ky], w=[kt1])
            fw.op("dve", lambda e, t1=t1: e.tensor_scalar(out=t1[:], in0=t1[:], scalar1=0.044715, scalar2=1.0, op0=ALU.mult, op1=ALU.add), r=[kt1], w=[kt1])
            fw.op("pool", lambda e, yt=yt, t1=t1: e.tensor_tensor(out=t1[:], in0=t1[:], in1=yt[:], op=ALU.mult), r=[kt1, ky], w=[kt1])
            fw.op("act", lambda e, t1=t1: e.activation(out=t1[:], in_=t1[:], func=AF.Sigmoid, scale=1.5957691216057308), r=[kt1], w=[kt1])
            fw.op("dve", lambda e, yt=yt, t1=t1, ct=ct: e.tensor_tensor(out=zT[:, ct, :].rearrange("p (n t) -> p t n", t=8),
                                                                     in0=yt[:].rearrange("p (t n) -> p t n", t=8),
                                                                     in1=t1[:].rearrange("p (t n) -> p t n", t=8), op=ALU.mult),
                  r=[ky, kt1], w=[("zT", ct)])
        pss = [P.ps(st, f"glps{i}", [128, 512]) for i in range(2)]
        sgs = [P.sb(st, f"glsg{i}", [128, 512], BF16) for i in range(2)]
        zk = [("zT", ct) for ct in range(8)]
        i = 0
        for cot in range(8):
            for blk in range(T // 512):
                ps = pss[i % 2]; sg = sgs[i % 2]; pk = ("glps", i % 2); sk = ("glsg", i % 2); i += 1
                cs = slice(blk * 512, (blk + 1) * 512)
                fw.op("pe", lambda e, ps=ps, cot=cot, cs=cs: _mm_group(e, ps[:], [(wg[:, kt, cot * 128:(cot + 1) * 128], zT[:, kt, cs]) for kt in range(8)]),
                      r=["wglu"] + zk, w=[pk])
                fw.op("act", lambda e, ps=ps, sg=sg, cot=cot: e.activation(out=sg[:], in_=ps[:], func=AF.Sigmoid, bias=bg[:, cot:cot + 1]), r=[pk, "s5bg"], w=[sk])
                fw.op("dve", lambda e, sg=sg, cot=cot, cs=cs: e.tensor_tensor(out=s5oT[:, cot, cs], in0=zT[:, cot, cs], in1=sg[:], op=ALU.mult), r=[sk] + zk, w=[("s5oT", cot)])
        fw.barrier()
    return s5oT


def _c(a):
    return np.ascontiguousarray(a, dtype=np.float32)


def s5_host(lam_re, lam_im, log_dt, b_re, b_im, c_re, c_im):
    L = lam_re.shape[0]
    q = np.arange(128)
    hq, tq = q // 8, q % 8
    out = {}
    out["kA"] = _c((7 - tq)[:, None])
    out["kC"] = _c(np.broadcast_to((tq + 1)[None, None, :], (128, 4, 128)))
    out["kF"] = _c(np.broadcast_to((tq - 7)[None, None, :], (128, 4, 128)))
    out["cmask"] = _c((tq[None, :] >= tq[:, None]))
    out["lrA"] = _c(np.broadcast_to(lam_re[:, None, :, :], (L, 128, 64, 64)))
    out["liA"] = _c(np.broadcast_to(lam_im[:, None, :, :], (L, 128, 64, 64)))
    out["ldA"] = _c(np.broadcast_to(log_dt[:, None, :, None], (L, 128, 64, 64)))
    out["bReA"] = _c(b_re.transpose(0, 3, 1, 2)[:, hq])
    out["bImA"] = _c(b_im.transpose(0, 3, 1, 2)[:, hq])
    def Bl(x_gp):
        return x_gp.reshape(L, 32, 2, 64).transpose(0, 2, 3, 1).reshape(L, 128, 32)
    lrC, liC = Bl(lam_re), Bl(lam_im)
    ldC = Bl(np.broadcast_to(log_dt[:, :, None], (L, 64, 64)))
    out["lrC"], out["liC"], out["ldC"] = _c(lrC), _c(liC), _c(ldC)
    out["lrB"] = _c(np.broadcast_to(lrC[:, :, :, None], (L, 128, 32, 128)))
    out["liB"] = _c(np.broadcast_to(liC[:, :, :, None], (L, 128, 32, 128)))
    out["ldB"] = _c(np.broadcast_to(ldC[:, :, :, None], (L, 128, 32, 128)))
    def Cl(c):
        x = c[:, :, hq, :]
        x = x.reshape(L, 32, 2, 128, 64)
        return x.transpose(0, 2, 4, 1, 3).reshape(L, 128, 32, 128)
    out["cReB"] = _c(Cl(c_re))
    out["cImB"] = _c(Cl(c_im))
    return out


def pk8(v):
    L = v.shape[0]
    return _c(v.reshape(L, -1, 128).transpose(0, 2, 1))


def gdn_consts_host():
    i = np.arange(128)
    same = (i[:, None] // 64) == (i[None, :] // 64)
    c = {}
    c["LT"] = _c(same & (i[:, None] <= i[None, :]))
    c["CHK"] = _c(same)
    c["SEL0"] = _c(np.broadcast_to((i < 64)[:, None], (128, 128)))
    c["SEL1"] = _c(np.broadcast_to((i >= 64)[:, None], (128, 128)))
    NEG = -30000.0
    c["mLs"] = _c(np.where(same & (i[None, :] < i[:, None]), 0.0, NEG))
    c["mUs"] = _c(np.where(same & (i[None, :] > i[:, None]), 0.0, NEG))
    c["mUi"] = _c(np.where(same & (i[None, :] >= i[:, None]), 0.0, NEG))
    c["ones"] = _c(np.ones((128, 128)))
    return c


GDN_CONST_NAMES = ("LT", "CHK", "SEL0", "SEL1", "mLs", "mUs", "mUi", "ones")


def phase_gdn_pre(P, l, QKV, BA, gin, gconst_in, ident_f, GL, UU, KT, ATT, WT, QDT):
    nc, fw = P.nc, P.fw
    T, NT = P.T, P.NT
    with ExitStack() as st:
        C = {}
        for nm in GDN_CONST_NAMES:
            C[nm] = P.sb(st, "gc_" + nm, [128, 128])
            fw.dma("sp", C[nm][:], gconst_in[nm], w=["gc_" + nm])
        ck = ["gc_" + nm for nm in GDN_CONST_NAMES]
        dtb = P.sb(st, "g_dtb", [128, 8]); nega = P.sb(st, "g_nega", [128, 8])
        fw.dma("sp", dtb[:], gin["dt_bias"][l:l + 1, :].partition_broadcast(128), w=["g_dtb"])
        fw.dma("sp", nega[:], gin["a_log"][l:l + 1, :].partition_broadcast(128), w=["g_nega"])
        fw.op("act", lambda e: e.activation(out=nega[:], in_=nega[:], func=AF.Exp), r=["g_nega"], w=["g_nega"])
        fw.op("dve", lambda e: e.tensor_scalar(out=nega[:], in0=nega[:], scalar1=-1.0, scalar2=None, op0=ALU.mult), r=["g_nega"], w=["g_nega"])
        big = lambda nm, dt=F32: P.sb(st, nm, [128, 8, 128], dt)
        qkv = P.sb(st, "g_qkv", [128, 24, 128]); ba = P.sb(st, "g_ba", [128, 16])
        sq = P.sb(st, "g_sq", [128, 16, 128]); ssq = P.sb(st, "g_ssq", [128, 16])
        qkn = P.sb(st, "g_qkn", [128, 16, 128])
        sm = {nm: P.sb(st, "g_" + nm, [128, 8]) for nm in ("beta", "lnb", "nbeta", "x", "g", "gc", "glb", "egc", "bg", "ekt", "gcb", "gl2")}
        Dg, Dgb, tA, tB, EL, EU, EUb = (big("g_" + n) for n in ("Dg", "Dgb", "tA", "tB", "EL", "EU", "EUb"))
        kT, qT = big("g_kT"), big("g_qT")
        Na, NTa, Nb, NTb, R = (big("g_" + n) for n in ("Na", "NTa", "Nb", "NTb", "R"))
        att, kbg, vb, Dq, ktl, stg = (big("g_" + n) for n in ("att", "kbg", "vb", "Dq", "ktl", "stg"))
        PS = [P.ps(st, f"g_ps{i}", [128, 8, 128]) for i in range(4)]
        pi = [0]

        def nps():
            b = pi[0] % 4
            pi[0] += 1
            return PS[b], ("g_ps", b)
        bc_h = lambda t: t[:].unsqueeze(2).to_broadcast([128, 8, 128])
        bc_m = lambda t: t[:].unsqueeze(1).to_broadcast([128, 8, 128])

        def headmm(ps, lhs_fn, rhs_fn):
            def f(e):
                ins = None
                for h in range(8):
                    ins = e.matmul(ps[:, h, :], lhs_fn(h), rhs_fn(h), start=True, stop=True)
                return ins
            return f

        for tt in range(NT):
            rows = slice(tt * 128, (tt + 1) * 128)
            fw.dma("sp", qkv[:].rearrange("p a b -> p (a b)"), QKV[rows, :], r=[("QKV", tt // 4, ct) for ct in range(24)], w=["qkv"])
            fw.dma("sp", ba[:], BA[rows, :], r=[(id(BA), tt, 0)], w=["ba"])
            fw.op("pool", lambda e: e.tensor_tensor(out=sq[:], in0=qkv[:, 0:16, :], in1=qkv[:, 0:16, :], op=ALU.mult), r=["qkv"], w=["sq"])
            fw.op("dve", lambda e: e.tensor_reduce(out=ssq[:], in_=sq[:], axis=AX.X, op=ALU.add), r=["sq"], w=["ssq"])
            rsqrt_eps(fw, ssq[:], ssq[:], 1e-6, ["ssq"], ["ssq"])
            fw.op("dve", lambda e: e.tensor_scalar(out=ssq[:, 0:8], in0=ssq[:, 0:8], scalar1=128.0 ** -0.5, scalar2=None, op0=ALU.mult), r=["ssq"], w=["ssq"])
            fw.op("dve", lambda e: e.tensor_tensor(out=qkn[:], in0=qkv[:, 0:16, :], in1=ssq[:].unsqueeze(2).to_broadcast([128, 16, 128]), op=ALU.mult),
                  r=["qkv", "ssq"], w=["qkn"])
            qh = lambda h: qkn[:, h, :]
            kh = lambda h: qkn[:, 8 + h, :]
            fw.op("act", lambda e: e.activation(out=sm["beta"][:], in_=ba[:, 0:8], func=AF.Sigmoid), r=["ba"], w=["beta"])
            fw.op("act", lambda e: e.activation(out=sm["lnb"][:], in_=sm["beta"][:], func=AF.Ln), r=["beta"], w=["lnb"])
            fw.op("dve", lambda e: e.tensor_scalar(out=sm["nbeta"][:], in0=sm["beta"][:], scalar1=-1.0, scalar2=None, op0=ALU.mult), r=["beta"], w=["nbeta"])
            fw.op("dve", lambda e: e.tensor_tensor(out=sm["x"][:], in0=ba[:, 8:16], in1=dtb[:], op=ALU.add), r=["ba", "g_dtb"], w=["x"])
            fw.op("act", lambda e: e.activation(out=sm["x"][:], in_=sm["x"][:], func=AF.Exp), r=["x"], w=["x"])
            fw.op("act", lambda e: e.activation(out=sm["x"][:], in_=sm["x"][:], func=AF.Ln, bias=1.0), r=["x"], w=["x"])
            fw.op("dve", lambda e: e.tensor_tensor(out=sm["g"][:], in0=sm["x"][:], in1=nega[:], op=ALU.mult), r=["x", "g_nega"], w=["g"])
            ps, pk = nps()

            def cs(e, ps=ps):
                e.matmul(ps[:, 0, 0:8], C["LT"][:], sm["g"][:], start=True, stop=True)
                e.matmul(ps[:, 1, 0:8], C["CHK"][:], sm["g"][:], start=True, stop=True)
                e.matmul(ps[:, 2, 0:8], C["SEL0"][:], sm["g"][:], start=True, stop=True)
                return e.matmul(ps[:, 3, 0:8], C["SEL1"][:], sm["g"][:], start=True, stop=True)
            fw.op("pe", cs, r=["g"] + ck, w=[pk])
            fw.op("dve", lambda e, ps=ps: e.tensor_copy(out=sm["gc"][:], in_=ps[:, 0, 0:8]), r=[pk], w=["gc"])
            fw.op("dve", lambda e, ps=ps: e.tensor_copy(out=sm["glb"][:], in_=ps[:, 1, 0:8]), r=[pk], w=["glb"])
            fw.op("dve", lambda e, ps=ps, tt=tt: e.tensor_copy(out=GL[:, tt, :, :], in_=ps[:, 2:4, 0:8]), r=[pk], w=[("GL", tt)])
            fw.op("act", lambda e, tt=tt: e.activation(out=GL[:, tt, :, :], in_=GL[:, tt, :, :], func=AF.Exp), r=[("GL", tt)], w=[("GL", tt)])
            fw.op("act", lambda e: e.activation(out=sm["egc"][:], in_=sm["gc"][:], func=AF.Exp), r=["gc"], w=["egc"])
            fw.op("dve", lambda e: e.tensor_tensor(out=sm["bg"][:], in0=sm["beta"][:], in1=sm["egc"][:], op=ALU.mult), r=["beta", "egc"], w=["bg"])
            fw.op("dve", lambda e: e.tensor_tensor(out=sm["ekt"][:], in0=sm["glb"][:], in1=sm["gc"][:], op=ALU.subtract), r=["glb", "gc"], w=["ekt"])
            fw.op("act", lambda e: e.activation(out=sm["ekt"][:], in_=sm["ekt"][:], func=AF.Exp), r=["ekt"], w=["ekt"])
            fw.op("dve", lambda e: e.tensor_tensor(out=sm["gcb"][:], in0=sm["gc"][:], in1=sm["lnb"][:], op=ALU.add), r=["gc", "lnb"], w=["gcb"])
            fw.op("dve", lambda e: e.tensor_tensor(out=Dg[:], in0=bc_m(ident_f), in1=bc_h(sm["gc"]), op=ALU.mult), r=["ident_f", "gc"], w=["Dg"])
            fw.op("pool", lambda e: e.tensor_tensor(out=Dgb[:], in0=bc_m(ident_f), in1=bc_h(sm["gcb"]), op=ALU.mult), r=["ident_f", "gcb"], w=["Dgb"])
            for (src, skey, dst, dkey) in ((Dg, "Dg", tA, "tA"), (Dgb, "Dgb", tB, "tB")):
                ps, pk = nps()

                def gr(e, ps=ps, src=src):
                    f2 = lambda t: t.rearrange("p a b -> p (a b)")
                    e.matmul(f2(ps[:, 0:4, :]), C["ones"][:], f2(src[:, 0:4, :]), start=True, stop=True)
                    return e.matmul(f2(ps[:, 4:8, :]), C["ones"][:], f2(src[:, 4:8, :]), start=True, stop=True)
                fw.op("pe", gr, r=[skey, "gc_ones"], w=[pk])
                fw.op("dve", lambda e, ps=ps, dst=dst: e.tensor_tensor(out=dst[:], in0=ps[:], in1=bc_h(sm["gc"]), op=ALU.subtract), r=[pk, "gc"], w=[dkey])
            fw.op("pool", lambda e: e.tensor_tensor(out=EL[:], in0=bc_m(C["mLs"]), in1=tA[:], op=ALU.subtract), r=["tA", "gc_mLs"], w=["EL"])
            fw.op("act", lambda e: e.activation(out=EL[:], in_=EL[:], func=AF.Exp), r=["EL"], w=["EL"])
            fw.op("pool", lambda e: e.tensor_tensor(out=EU[:], in0=bc_m(C["mUi"]), in1=tA[:], op=ALU.add), r=["tA", "gc_mUi"], w=["EU"])
            fw.op("act", lambda e: e.activation(out=EU[:], in_=EU[:], func=AF.Exp), r=["EU"], w=["EU"])
            fw.op("pool", lambda e: e.tensor_tensor(out=EUb[:], in0=bc_m(C["mUs"]), in1=tB[:], op=ALU.add), r=["tB", "gc_mUs"], w=["EUb"])
            fw.op("act", lambda e: e.activation(out=EUb[:], in_=EUb[:], func=AF.Exp), r=["EUb"], w=["EUb"])
            for (fn, dst, dkey) in ((kh, kT, "kT"), (qh, qT, "qT")):
                ps, pk = nps()

                def trf(e, ps=ps, fn=fn):
                    ins = None
                    for h in range(8):
                        ins = e.transpose(out=ps[:, h, :], in_=fn(h), identity=ident_f[:])
                    return ins
                fw.op("pe", trf, r=["qkn", "ident_f"], w=[pk])
                if dkey == "kT":
                    fw.op("dve", lambda e, ps=ps, dst=dst: e.tensor_copy(out=dst[:], in_=ps[:]), r=[pk], w=[dkey])
                else:
                    fw.op("act", lambda e, ps=ps, dst=dst: e.copy(out=dst[:], in_=ps[:]), r=[pk], w=[dkey])
            psS, pkS = nps()
            fw.op("pe", headmm(psS, lambda h: kT[:, h, :], lambda h: kT[:, h, :]), r=["kT"], w=[pkS])
            fw.op("dve", lambda e, psS=psS: e.tensor_tensor(out=NTa[:], in0=psS[:], in1=EL[:], op=ALU.mult), r=[pkS, "EL"], w=["NTa"])
            fw.op("pool", lambda e: e.tensor_tensor(out=NTa[:], in0=NTa[:], in1=bc_h(sm["nbeta"]), op=ALU.mult), r=["NTa", "nbeta"], w=["NTa"])
            fw.op("dve", lambda e, psS=psS: e.tensor_tensor(out=Na[:], in0=psS[:], in1=EUb[:], op=ALU.mult), r=[pkS, "EUb"], w=["Na"])
            fw.op("pool", lambda e: e.tensor_scalar(out=Na[:], in0=Na[:], scalar1=-1.0, scalar2=None, op0=ALU.mult), r=["Na"], w=["Na"])
            psQ, pkQ = nps()
            fw.op("pe", headmm(psQ, lambda h: kT[:, h, :], lambda h: qT[:, h, :]), r=["kT", "qT"], w=[pkQ])
            fw.op("dve", lambda e, psQ=psQ: e.tensor_tensor(out=att[:], in0=psQ[:], in1=EU[:], op=ALU.mult), r=[pkQ, "EU"], w=["att"])
            fw.dma("sp", ATT[tt], att[:], r=["att"], w=[("ATT", tt)])
            fw.op("pool", lambda e: e.tensor_tensor(out=R[:], in0=Na[:], in1=bc_m(ident_f), op=ALU.add), r=["Na", "ident_f"], w=["R"])
            cur = (Na, "Na", NTa, "NTa"); oth = (Nb, "Nb", NTb, "NTb")
            for lvl in range(1, 6):
                N_, nk, NT_, ntk = cur
                N2, n2k, NT2, nt2k = oth
                if lvl < 5:
                    ps1, pk1 = nps()
                    fw.op("pe", headmm(ps1, lambda h, NT_=NT_: NT_[:, h, :], lambda h, N_=N_: N_[:, h, :]), r=[nk, ntk], w=[pk1])
                    fw.op("dve", lambda e, ps1=ps1, N2=N2: e.tensor_copy(out=N2[:], in_=ps1[:]), r=[pk1], w=[n2k])
                ps2, pk2 = nps()
                fw.op("pe", headmm(ps2, lambda h, N_=N_: N_[:, h, :], lambda h, NT_=NT_: NT_[:, h, :]), r=[nk, ntk], w=[pk2])
                fw.op("act", lambda e, ps2=ps2, NT2=NT2: e.copy(out=NT2[:], in_=ps2[:]), r=[pk2], w=[nt2k])
                ps3, pk3 = nps()
                fw.op("pe", headmm(ps3, lambda h, NT2=NT2: NT2[:, h, :], lambda h: R[:, h, :]), r=[nt2k, "R"], w=[pk3])
                fw.op("dve", lambda e, ps3=ps3: e.tensor_tensor(out=R[:], in0=ps3[:], in1=R[:], op=ALU.add), r=[pk3, "R"], w=["R"])
                cur, oth = oth, cur
            fw.op("pool", lambda e: e.tensor_tensor(out=kbg[:], in0=qkn[:, 8:16, :], in1=bc_h(sm["bg"]), op=ALU.mult), r=["qkn", "bg"], w=["kbg"])
            fw.op("pool", lambda e: e.tensor_tensor(out=vb[:], in0=qkv[:, 16:24, :], in1=bc_h(sm["beta"]), op=ALU.mult), r=["qkv", "beta"], w=["vb"])
            fw.op("pool", lambda e: e.tensor_tensor(out=Dq[:], in0=bc_m(ident_f), in1=bc_h(sm["egc"]), op=ALU.mult), r=["ident_f", "egc"], w=["Dq"])
            fw.op("pool", lambda e: e.tensor_tensor(out=ktl[:], in0=qkn[:, 8:16, :], in1=bc_h(sm["ekt"]), op=ALU.mult), r=["qkn", "ekt"], w=["ktl"])
            fw.dma("sp", KT[rows, :], ktl[:].rearrange("p a b -> p (a b)"), r=["ktl"], w=[("KT", tt)])
            psu, pku = nps()
            fw.op("pe", headmm(psu, lambda h: R[:, h, :], lambda h: vb[:, h, :]), r=["R", "vb"], w=[pku])
            fw.op("act", lambda e, psu=psu: e.copy(out=stg[:], in_=psu[:]), r=[pku], w=["stg"])
            fw.dma("sp", UU[rows, :], stg[:].rearrange("p a b -> p (a b)"), r=["stg"], w=[("UU", tt)])
            psw, pkw = nps()
            fw.op("pe", headmm(psw, lambda h: kbg[:, h, :], lambda h: R[:, h, :]), r=["R", "kbg"], w=[pkw])
            fw.op("dve", lambda e, psw=psw: e.tensor_copy(out=EL[:], in_=psw[:]), r=[pkw], w=["EL"])
            fw.dma("sp", WT[tt], EL[:], r=["EL"], w=[("WT", tt)])
            psq, pkq = nps()
            fw.op("pe", headmm(psq, lambda h: qkn[:, h, :], lambda h: Dq[:, h, :]), r=["qkn", "Dq"], w=[pkq])
            fw.op("act", lambda e, psq=psq: e.copy(out=EU[:], in_=psq[:]), r=[pkq], w=["EU"])
            fw.dma("sp", QDT[tt], EU[:], r=["EU"], w=[("QDT", tt)])
        fw.barrier()


def _gdn_chunk_loads(P, fw, st, ci, UU, KT, WT, ATT, QDT, Zd, full):
    tt, cc = ci // 2, ci % 2
    b = ci % 2
    r0 = 64 * ci
    cs = slice(64 * cc, 64 * cc + 64)
    out = {}
    uu = P.sb(st, f"c_uu{b}", [64, 8, 128]); kt = P.sb(st, f"c_kt{b}", [64, 8, 128]); wt = P.sb(st, f"c_wt{b}", [128, 8, 64])
    fw.dma("sp", uu[:].rearrange("p a b -> p (a b)"), UU[r0:r0 + 64, :], r=[("UU", tt)], w=[("c_uu", b)])
    fw.dma("sp", kt[:].rearrange("p a b -> p (a b)"), KT[r0:r0 + 64, :], r=[("KT", tt)], w=[("c_kt", b)])
    fw.dma("sp", wt[:], WT[tt][:, :, cs], r=[("WT", tt)], w=[("c_wt", b)])
    out["uu"], out["kt"], out["wt"] = (uu, ("c_uu", b)), (kt, ("c_kt", b)), (wt, ("c_wt", b))
    if full:
        at = P.sb(st, f"c_at{b}", [64, 8, 64]); qd = P.sb(st, f"c_qd{b}", [128, 8, 64]); z = P.sb(st, f"c_z{b}", [64, 1024])
        fw.dma("sp", at[:], ATT[tt][cs, :, cs], r=[("ATT", tt)], w=[("c_at", b)])
        fw.dma("sp", qd[:], QDT[tt][:, :, cs], r=[("QDT", tt)], w=[("c_qd", b)])
        fw.dma("sp", z[:], Zd[r0:r0 + 64, :], w=[("c_z", b)])
        out["at"], out["qd"], out["z"] = (at, ("c_at", b)), (qd, ("c_qd", b)), (z, ("c_z", b))
    return out


def phase_gdn_pass1(P, l, GL, UU, KT, WT, ident_f, GB, GG):
    nc, fw = P.nc, P.fw
    NC2 = 2 * P.NT
    with ExitStack() as st:
        SA = P.sb(st, "SA", [128, 8, 256])
        fw.op("pool", lambda e: e.memset(SA[:], 0.0), w=[("SA", h) for h in range(8)])
        fw.op("pool", lambda e: e.tensor_copy(out=SA[:, :, 128:256], in_=ident_f[:].unsqueeze(1).to_broadcast([128, 8, 128])),
              r=["ident_f"], w=[("SA", h) for h in range(8)])
        ps1s = [P.ps(st, f"p1a{i}", [128, 512]) for i in range(4)]
        ps2s = [P.ps(st, f"p1b{i}", [128, 512]) for i in range(4)]
        vns = [P.sb(st, f"p1vn{i}", [64, 256]) for i in range(4)]
        i = 0
        for ci in range(NC2):
            tt, cc = ci // 2, ci % 2
            L = _gdn_chunk_loads(P, fw, st, ci, UU, KT, WT, None, None, None, False)
            (uu, uk), (kt, kk), (wt, wk) = L["uu"], L["kt"], L["wt"]
            for h in range(8):
                b = i % 4; i += 1
                p1, p2, vn = ps1s[b], ps2s[b], vns[b]
                k1, k2, kv = ("p1a", b), ("p1b", b), ("p1vn", b)
                fw.op("pe", lambda e, p1=p1, wt=wt, h=h: e.matmul(p1[0:64, 0:256], wt[:, h, :], SA[:, h, :], start=True, stop=True), r=[wk, ("SA", h)], w=[k1])
                fw.op("dve", lambda e, p1=p1, vn=vn, uu=uu, h=h: e.tensor_tensor(out=vn[:, 0:128], in0=uu[:, h, :], in1=p1[0:64, 0:128], op=ALU.subtract), r=[k1, uk], w=[kv])
                fw.op("dve", lambda e, p1=p1, vn=vn: e.tensor_scalar(out=vn[:, 128:256], in0=p1[0:64, 128:256], scalar1=-1.0, scalar2=None, op0=ALU.mult), r=[k1], w=[(kv, 1)])
                fw.op("pe", lambda e, p2=p2, kt=kt, vn=vn, h=h: e.matmul(p2[:, 0:256], kt[:, h, :], vn[:, :], start=True, stop=True), r=[kk, kv, (kv, 1)], w=[k2])
                fw.op("dve", lambda e, p2=p2, h=h, tt=tt, cc=cc: e.scalar_tensor_tensor(out=SA[:, h, :], in0=SA[:, h, :], scalar=GL[:, tt, cc, h:h + 1], in1=p2[:, 0:256], op0=ALU.mult, op1=ALU.add),
                      r=[k2, ("SA", h), ("GL", tt)], w=[("SA", h)])
        fw.dma("sp", GB.rearrange("(h d) c -> d h c", h=8), SA[:], r=[("SA", h) for h in range(8)], w=["GB"])
        fw.allgather(GG.opt(), GB.opt(), r=["GB"], w=["GG"], ncores=P.ncores)
        fw.barrier()


def phase_gdn_chain(P, st_keep, GG, mask, ident_f):
    nc, fw = P.nc, P.fw
    S0 = P.sb(st_keep, "gS", [128, 8, 128])
    with ExitStack() as st:
        acc = P.sb(st, "ch_acc", [128, 8, 128]); PT = P.sb(st, "ch_PT", [128, 8, 128])
        qps = [P.sb(st, f"ch_qp{i}", [128, 8, 256]) for i in range(2)]
        pss = [P.ps(st, f"ch_ps{i}", [128, 8, 128]) for i in range(2)]
        SK = [("gS", h) for h in range(8)]
        fw.op("pool", lambda e: e.memset(S0[:], 0.0), w=SK)
        for r_ in range(P.ncores - 1):
            qp = qps[r_ % 2]; qk = ("ch_qp", r_ % 2)
            fw.dma("sp", qp[:], GG[r_ * 1024:(r_ + 1) * 1024, :].rearrange("(h d) c -> d h c", h=8), r=["GG"], w=[qk])
            if r_ == 0:
                fw.op("dve", lambda e, qp=qp: e.tensor_copy(out=acc[:], in_=qp[:, :, 0:128]), r=[qk], w=["ch_acc"])
            else:
                def trf(e, qp=qp):
                    ins = None
                    for h in range(8):
                        ins = e.transpose(out=pss[0][:, h, :], in_=qp[:, h, 128:256], identity=ident_f[:])
                    return ins
                fw.op("pe", trf, r=[qk, "ident_f"], w=["ch_ps0"])
                fw.op("act", lambda e: e.copy(out=PT[:], in_=pss[0][:]), r=["ch_ps0"], w=["ch_PT"])

                def mm(e):
                    ins = None
                    for h in range(8):
                        ins = e.matmul(pss[1][:, h, :], PT[:, h, :], acc[:, h, :], start=True, stop=True)
                    return ins
                fw.op("pe", mm, r=["ch_PT", "ch_acc"], w=["ch_ps1"])
                fw.op("dve", lambda e, qp=qp: e.tensor_tensor(out=acc[:], in0=pss[1][:], in1=qp[:, :, 0:128], op=ALU.add), r=["ch_ps1", qk], w=["ch_acc"])
            fw.op("dve", lambda e, r_=r_: e.scalar_tensor_tensor(out=S0[:], in0=acc[:], scalar=mask[:, r_ + 1:r_ + 2], in1=S0[:], op0=ALU.mult, op1=ALU.add),
                  r=["ch_acc", "mask"] + SK, w=SK)
        fw.barrier()
    return S0


def phase_gdn_pass2(P, l, st_keep, S, GL, UU, KT, WT, ATT, QDT, Zd, normw_in, ident_bf):
    nc, fw = P.nc, P.fw
    NC2 = 2 * P.NT
    gdnT = P.sb(st_keep, "gdnT", [128, 8, P.T], BF16)
    with ExitStack() as st:
        nw = P.sb(st, "g_nw", [64, 128])
        fw.dma("sp", nw[:], normw_in[l:l + 1, :].partition_broadcast(64), w=["g_nw"])
        ps1s = [P.ps(st, f"p2a{i}", [128, 512]) for i in range(2)]
        psos = [P.ps(st, f"p2o{i}", [128, 512]) for i in range(2)]
        ps2s = [P.ps(st, f"p2b{i}", [128, 512]) for i in range(2)]
        pst = P.ps(st, "p2t", [128, 8, 128], BF16)
        vns = [P.sb(st, f"p2vn{i}", [64, 128]) for i in range(4)]
        Os = [P.sb(st, f"p2O{i}", [64, 8, 128]) for i in range(2)]
        sqo = P.sb(st, "p2sq", [64, 8, 128]); sso = P.sb(st, "p2ss", [64, 8]); sz = P.sb(st, "p2sz", [64, 8, 128])
        gd = P.sb(st, "p2gd", [64, 8, 128], BF16)
        i = 0
        for ci in range(NC2):
            tt, cc = ci // 2, ci % 2
            L = _gdn_chunk_loads(P, fw, st, ci, UU, KT, WT, ATT, QDT, Zd, True)
            (uu, uk), (kt, kk), (wt, wk), (at, ak), (qd, qk), (z, zk) = L["uu"], L["kt"], L["wt"], L["at"], L["qd"], L["z"]
            O = Os[ci % 2]; ok_ = ("p2O", ci % 2)
            for h in range(8):
                b2 = i % 2; b4 = i % 4; i += 1
                p1, po, p2, vn = ps1s[b2], psos[b2], ps2s[b2], vns[b4]
                k1, ko, k2, kv = ("p2a", b2), ("p2o", b2), ("p2b", b2), ("p2vn", b4)
                fw.op("pe", lambda e, p1=p1, wt=wt, h=h: e.matmul(p1[0:64, 0:128], wt[:, h, :], S[:, h, :], start=True, stop=True), r=[wk, ("gS", h)], w=[k1])
                fw.op("dve", lambda e, p1=p1, vn=vn, uu=uu, h=h: e.tensor_tensor(out=vn[:], in0=uu[:, h, :], in1=p1[0:64, 0:128], op=ALU.subtract), r=[k1, uk], w=[kv])

                def om(e, po=po, qd=qd, at=at, vn=vn, h=h):
                    e.matmul(po[0:64, 0:128], qd[:, h, :], S[:, h, :], start=True, stop=False)
                    return e.matmul(po[0:64, 0:128], at[:, h, :], vn[:], start=False, stop=True)
                fw.op("pe", om, r=[qk, ak, kv, ("gS", h)], w=[ko])
                fw.op("pe", lambda e, p2=p2, kt=kt, vn=vn, h=h: e.matmul(p2[:, 0:128], kt[:, h, :], vn[:], start=True, stop=True), r=[kk, kv], w=[k2])
                fw.op("dve", lambda e, p2=p2, h=h, tt=tt, cc=cc: e.scalar_tensor_tensor(out=S[:, h, :], in0=S[:, h, :], scalar=GL[:, tt, cc, h:h + 1], in1=p2[:, 0:128], op0=ALU.mult, op1=ALU.add),
                      r=[k2, ("gS", h), ("GL", tt)], w=[("gS", h)])
                fw.op("act", lambda e, po=po, O=O, h=h: e.copy(out=O[:, h, :], in_=po[0:64, 0:128]), r=[ko], w=[(ok_, h)])
            okeys = [(ok_, h) for h in range(8)]
            fw.op("pool", lambda e, O=O: e.tensor_tensor(out=sqo[:], in0=O[:], in1=O[:], op=ALU.mult), r=okeys, w=["p2sq"])
            fw.op("dve", lambda e: e.tensor_reduce(out=sso[:], in_=sqo[:], axis=AX.X, op=ALU.add), r=["p2sq"], w=["p2ss"])
            rsqrt_eps(fw, sso[:], sso[:], 1e-6, ["p2ss"], ["p2ss"], scale=1.0 / 128.0)
            fw.op("dve", lambda e, O=O: e.tensor_tensor(out=sqo[:], in0=O[:], in1=sso[:].unsqueeze(2).to_broadcast([64, 8, 128]), op=ALU.mult), r=okeys + ["p2ss", "p2sq"], w=["p2sq"])
            fw.op("pool", lambda e: e.tensor_tensor(out=sqo[:], in0=sqo[:], in1=nw[:].unsqueeze(1).to_broadcast([64, 8, 128]), op=ALU.mult), r=["p2sq", "g_nw"], w=["p2sq"])
            fw.op("act", lambda e, z=z: e.activation(out=sz[:].rearrange("p a b -> p (a b)"), in_=z[:], func=AF.Silu), r=[zk], w=["p2sz"])
            fw.op("dve", lambda e: e.tensor_tensor(out=gd[:], in0=sqo[:], in1=sz[:], op=ALU.mult), r=["p2sq", "p2sz"], w=["p2gd"])

            def trf(e):
                ins = None
                for h in range(8):
                    ins = e.transpose(out=pst[:, h, 0:64], in_=gd[:, h, :], identity=ident_bf[0:64, 0:64])
                return ins
            fw.op("pe", trf, r=["p2gd", "ident_bf"], w=["p2t"])
            fw.op("act", lambda e, ci=ci: e.copy(out=gdnT[:, :, 64 * ci:64 * ci + 64], in_=pst[:, :, 0:64]), r=["p2t"], w=[("gdnT", ci)])
        fw.barrier()
    return gdnT


def phase_merge(P, l, st_keep, s5oT, gdnT, w_up_ssm, w_up_gdn, G, ident_bf):
    nc, fw = P.nc, P.fw
    T, NT = P.T, P.NT
    MT = P.sb(st_keep, "MT", [128, 16, T], BF16)
    with ExitStack() as st:
        wss = [P.sb(st, f"m_ws{i}", [128, 8, 512], BF16) for i in range(2)]
        wgs = [P.sb(st, f"m_wg{i}", [128, 8, 512], BF16) for i in range(2)]
        gss = [P.sb(st, f"m_gs{i}", [128, 512]) for i in range(2)]
        ggs = [P.sb(st, f"m_gg{i}", [128, 512]) for i in range(2)]
        m1s = [P.sb(st, f"m_m1{i}", [128, 512]) for i in range(2)]
        m2s = [P.sb(st, f"m_m2{i}", [128, 512]) for i in range(2)]
        mbs = [P.sb(st, f"m_mb{i}", [128, 512], BF16) for i in range(2)]
        pss = [P.ps(st, f"m_pss{i}", [128, 512]) for i in range(2)]
        psg = [P.ps(st, f"m_psg{i}", [128, 512]) for i in range(2)]
        pst = [P.ps(st, f"m_pst{i}", [128, 8, 128], BF16) for i in range(2)]
        skeys = [("s5oT", c) for c in range(8)]
        i = 0
        for cb in range(4):
            ws, wg = wss[cb % 2], wgs[cb % 2]
            wk, gk = ("m_ws", cb % 2), ("m_wg", cb % 2)
            cs = slice(cb * 512, (cb + 1) * 512)
            fw.dma("pool", ws[:], w_up_ssm[l, :, cs].rearrange("(kt p) n -> p kt n", p=128), w=[wk])
            fw.dma("pool", wg[:], w_up_gdn[l, :, cs].rearrange("(kt p) n -> p kt n", p=128), w=[gk])
            for tt in range(NT):
                b = i % 2; i += 1
                ts = slice(tt * 128, (tt + 1) * 128)
                fw.dma("sp", gss[b][:], G[ts, cb * 512:(cb + 1) * 512], w=[("m_gs", b)])
                fw.dma("sp", ggs[b][:], G[ts, 2048 + cb * 512:2048 + (cb + 1) * 512], w=[("m_gg", b)])
                fw.op("pe", lambda e, b=b, ws=ws, ts=ts: _mm_group(e, pss[b][:], [(s5oT[:, kt, ts], ws[:, kt, :]) for kt in range(8)]), r=[wk] + skeys, w=[("m_pss", b)])
                fw.op("pe", lambda e, b=b, wg=wg, ts=ts, tt=tt: _mm_group(e, psg[b][:], [(gdnT[:, kt, ts], wg[:, kt, :]) for kt in range(8)]),
                      r=[gk, ("gdnT", 2 * tt), ("gdnT", 2 * tt + 1)], w=[("m_psg", b)])
                fw.op("dve", lambda e, b=b: e.tensor_tensor(out=m1s[b][:], in0=pss[b][:], in1=gss[b][:], op=ALU.mult), r=[("m_pss", b), ("m_gs", b)], w=[("m_m1", b)])
                fw.op("dve", lambda e, b=b: e.tensor_tensor(out=m2s[b][:], in0=psg[b][:], in1=ggs[b][:], op=ALU.mult), r=[("m_psg", b), ("m_gg", b)], w=[("m_m2", b)])
                fw.op("pool", lambda e, b=b: e.tensor_tensor(out=mbs[b][:], in0=m1s[b][:], in1=m2s[b][:], op=ALU.add), r=[("m_m1", b), ("m_m2", b)], w=[("m_mb", b)])

                def trf(e, b=b):
                    ins = None
                    for q in range(4):
                        ins = e.transpose(out=pst[b][:, q, :], in_=mbs[b][:, q * 128:(q + 1) * 128], identity=ident_bf[:])
                    return ins
                fw.op("pe", trf, r=[("m_mb", b), "ident_bf"], w=[("m_pst", b)])
                fw.op("act", lambda e, b=b, cb=cb, ts=ts: e.copy(out=MT[:, cb * 4:(cb + 1) * 4, ts], in_=pst[b][:, 0:4, :]), r=[("m_pst", b)], w=[("MT", tt)])
        fw.barrier()
    return MT


def proj_tokmajor(P, AT, akeys_fn, KT, tok0, ntt, Wd, ncols, cbw, Y, yrow0, tag):
    nc, fw = P.nc, P.fw
    with ExitStack() as st:
        wbs = [P.sb(st, f"{tag}_w{i}", [128, KT, cbw], BF16) for i in range(2)]
        sgs = [P.sb(st, f"{tag}_s{i}", [128, cbw]) for i in range(3)]
        pss = [P.ps(st, f"{tag}_p{i}", [128, 512]) for i in range(4)]
        i = 0
        for cb in range(ncols // cbw):
            wb = wbs[cb % 2]; wk = (tag + "_w", cb % 2)
            fw.dma("pool", wb[:], Wd[:, cb * cbw:(cb + 1) * cbw].rearrange("(kt p) n -> p kt n", p=128), w=[wk])
            for tt in range(ntt):
                pb = i % 4; sb_ = i % 3; i += 1
                ts = slice(tok0 + tt * 128, tok0 + (tt + 1) * 128)
                fw.op("pe", lambda e, pb=pb, wb=wb, ts=ts: _mm_group(e, pss[pb][:, 0:cbw], [(AT[:, kt, ts], wb[:, kt, :]) for kt in range(KT)]),
                      r=[wk] + akeys_fn(tt), w=[(tag + "_p", pb)])
                if i % 2 == 0:
                    fw.op("act", lambda e, pb=pb, sb_=sb_: e.copy(out=sgs[sb_][:], in_=pss[pb][:, 0:cbw]), r=[(tag + "_p", pb)], w=[(tag + "_s", sb_)])
                else:
                    fw.op("dve", lambda e, pb=pb, sb_=sb_: e.tensor_copy(out=sgs[sb_][:], in_=pss[pb][:, 0:cbw]), r=[(tag + "_p", pb)], w=[(tag + "_s", sb_)])
                r0 = yrow0 + tt * 128
                fw.dma("sp", Y[r0:r0 + 128, cb * cbw:(cb + 1) * cbw], sgs[sb_][:], r=[(tag + "_s", sb_)], w=[(tag + "Y", r0, cb)])
        fw.barrier()


def phase_resid_ln(P, l, xsrc, Y, MODB, ig, lng, lnb, dst):
    nc, fw = P.nc, P.fw
    with ExitStack() as st:
        gm = load_mod(P, st, MODB, l, ig, "r_gm")
        gain = P.sb(st, "r_gain", [128, D]); bias = P.sb(st, "r_bias", [128, D])
        fw.dma("sp", gain[:], lng[l:l + 1, :].partition_broadcast(128), w=["r_gain"])
        fw.dma("sp", bias[:], lnb[l:l + 1, :].partition_broadcast(128), w=["r_bias"])
        xts = [P.sb(st, f"r_x{i}", [128, D]) for i in range(2)]
        yts = [P.sb(st, f"r_y{i}", [128, D]) for i in range(2)]
        stats = [P.sb(st, f"r_st{i}", [128, 4, 6]) for i in range(2)]
        mvs = [P.sb(st, f"r_mv{i}", [128, 2]) for i in range(2)]
        rss = [P.sb(st, f"r_rs{i}", [128, 1]) for i in range(2)]
        nms = [P.sb(st, f"r_nm{i}", [128, 1]) for i in range(2)]
        for tt in range(P.NT):
            b = tt % 2
            xt, yt = xts[b], yts[b]
            k = f"r{b}"
            rows = slice(tt * 128, (tt + 1) * 128)
            fw.dma("sp", xt[:], xsrc[rows, :], w=[k + "x"])
            fw.dma("sp", yt[:], Y[rows, :], w=[k + "y"])
            fw.op("pool", lambda e, yt=yt: e.tensor_tensor(out=yt[:], in0=yt[:], in1=gm[:], op=ALU.mult), r=[k + "y", "r_gm"], w=[k + "y"])
            fw.op("dve", lambda e, xt=xt, yt=yt: e.scalar_tensor_tensor(out=yt[:], in0=xt[:], scalar=float(ALPHA), in1=yt[:], op0=ALU.mult, op1=ALU.add),
                  r=[k + "x", k + "y"], w=[k + "y"])
            ln_stats(P, fw, yt, k + "y", stats[b], mvs[b], rss[b], nms[b], k)
            fw.op("act", lambda e, xt=xt, yt=yt, b=b: e.activation(out=xt[:], in_=yt[:], func=AF.Identity, bias=nms[b][:], scale=rss[b][:]),
                  r=[k + "y", k + "rs", k + "nm", k + "x"], w=[k + "x"])
            fw.op("pool", lambda e, xt=xt: e.tensor_tensor(out=xt[:], in0=xt[:], in1=gain[:], op=ALU.mult), r=[k + "x", "r_gain"], w=[k + "x"])
            fw.op("dve", lambda e, xt=xt: e.tensor_tensor(out=xt[:], in0=xt[:], in1=bias[:], op=ALU.add), r=[k + "x", "r_bias"], w=[k + "x"])
            fw.dma("sp", dst[rows, :], xt[:], r=[k + "x"], w=[("dst", tt)])
        fw.barrier()


def phase_ffn_in(P, l, h2T, ffn_w_in, ACTT):
    nc, fw = P.nc, P.fw
    T = P.T
    with ExitStack() as st:
        wgs = [P.sb(st, f"f_wg{i}", [128, 16, 512], BF16) for i in range(2)]
        wus = [P.sb(st, f"f_wu{i}", [128, 16, 512], BF16) for i in range(2)]
        psg = [P.ps(st, f"f_pg{i}", [128, 512]) for i in range(2)]
        psu = [P.ps(st, f"f_pu{i}", [128, 512]) for i in range(2)]
        sil = [P.sb(st, f"f_si{i}", [128, 512]) for i in range(2)]
        acb = [P.sb(st, f"f_ab{i}", [128, 512], BF16) for i in range(3)]
        hk = [("h2T", tt * 128) for tt in range(P.NT)]
        i = 0
        for g4 in range(FH // 512):
            wg, wu = wgs[g4 % 2], wus[g4 % 2]
            gk, uk = ("f_wg", g4 % 2), ("f_wu", g4 % 2)
            fw.dma("pool", wg[:], ffn_w_in[l, :, g4 * 512:(g4 + 1) * 512].rearrange("(kt p) n -> p kt n", p=128), w=[gk])
            fw.dma("pool", wu[:], ffn_w_in[l, :, FH + g4 * 512:FH + (g4 + 1) * 512].rearrange("(kt p) n -> p kt n", p=128), w=[uk])
            for jj in range(4):
                j = g4 * 4 + jj
                js = slice(jj * 128, (jj + 1) * 128)
                for blk in range(T // 512):
                    b = i % 2; a3 = i % 3; i += 1
                    cs = slice(blk * 512, (blk + 1) * 512)
                    fw.op("pe", lambda e, b=b, wg=wg, js=js, cs=cs: _mm_group(e, psg[b][:], [(wg[:, kt, js], h2T[:, kt, cs]) for kt in range(16)]), r=[gk] + hk, w=[("f_pg", b)])
                    fw.op("pe", lambda e, b=b, wu=wu, js=js, cs=cs: _mm_group(e, psu[b][:], [(wu[:, kt, js], h2T[:, kt, cs]) for kt in range(16)]), r=[uk] + hk, w=[("f_pu", b)])
                    fw.op("act", lambda e, b=b: e.activation(out=sil[b][:], in_=psg[b][:], func=AF.Silu), r=[("f_pg", b)], w=[("f_si", b)])
                    fw.op("dve", lambda e, b=b, a3=a3: e.tensor_tensor(out=acb[a3][:], in0=sil[b][:], in1=psu[b][:], op=ALU.mult), r=[("f_si", b), ("f_pu", b)], w=[("f_ab", a3)])
                    fw.dma("sp", ACTT[j][:, cs], acb[a3][:], r=[("f_ab", a3)], w=[("ACTT", j, blk)])
        fw.barrier()


def phase_ffn_out(P, l, ACTT, ffn_w_out, YF):
    nc, fw = P.nc, P.fw
    T = P.T
    nh = 2 if T >= 2048 else 1
    TH = T // nh
    KT = FH // 128
    for half in range(nh):
        with ExitStack() as st:
            at = P.sb(st, "fo_at", [128, KT, TH], BF16)
            for j0 in range(0, KT, 11):
                fw.dma("sp", at[:, j0:j0 + 11, :], ACTT[j0:j0 + 11, :, half * TH:(half + 1) * TH].rearrange("j p t -> p j t"), w=[("fo_at", j0)])
            ak = [("fo_at", j0) for j0 in range(0, KT, 11)]
            proj_tokmajor(P, at, lambda tt: ak, KT, 0, TH // 128, ffn_w_out[l], D, 256, YF, half * TH, "fo")


def phase_halo(P, XOUT, HB, HG, maskp, XH):
    nc, fw = P.nc, P.fw
    T = P.T
    with ExitStack() as st:
        t3 = P.sb(st, "h_t3", [3, D])
        fw.dma("sp", t3[:], XOUT[T - 3:T, :], w=["h_t3"])
        fw.dma("sp", HB[:, :], t3[:], r=["h_t3"], w=["HB"])
        fw.allgather(HG.opt(), HB.opt(), r=["HB"], w=["HG"], ncores=P.ncores)
        hg = P.sb(st, "h_hg", [3, P.ncores, D]); acc = P.sb(st, "h_acc", [3, D])
        fw.dma("sp", hg[:], HG.rearrange("(r p) f -> p r f", p=3), r=["HG"], w=["h_hg"])
        fw.op("dve", lambda e: e.memset(acc[:], 0.0), w=["h_acc"])
        for r_ in range(P.ncores):
            fw.op("dve", lambda e, r_=r_: e.scalar_tensor_tensor(out=acc[:], in0=hg[:, r_, :], scalar=maskp[0:3, r_:r_ + 1], in1=acc[:], op0=ALU.mult, op1=ALU.add),
                  r=["h_hg", "h_acc", "maskp"], w=["h_acc"])
        fw.dma("sp", XH[:, :], acc[:], r=["h_acc"], w=["XH"])
        fw.barrier()


_STOP = [False]


def _ck(name):
    if os.environ.get("FULL_STOP", "") == name:
        _STOP[0] = True


def _ph(fn, *a, **k):
    if _STOP[0]:
        return None
    return fn(*a, **k)


S5_SHAPES = {"kA": [128, 1], "kC": [128, 4, 128], "kF": [128, 4, 128], "cmask": [128, 128],
             "lrA": [128, 64, 64], "liA": [128, 64, 64], "ldA": [128, 64, 64], "bReA": [128, 64, 64], "bImA": [128, 64, 64],
             "lrB": [128, 32, 128], "liB": [128, 32, 128], "ldB": [128, 32, 128], "cReB": [128, 32, 128], "cImB": [128, 32, 128],
             "lrC": [128, 32], "liC": [128, 32], "ldC": [128, 32]}
S5_CONST = ("kA", "kC", "kF", "cmask")


def build(T, ncores, depth):
    P = Prog(T, ncores=ncores, depth=depth)
    nc, fw = P.nc, P.fw
    L = depth
    NT, N8 = P.NT, P.N8
    x_in = P.din("x", [T, D]); xh_in = P.din("xh", [3, D])
    c_pk = P.din("c_pk", [128, 16]); w_ada = P.din("w_ada", [L, D, 12288]); b_ada = P.din("b_ada", [L, 12288])
    w_in = P.din("w_in", [L, D, PIN]); cw_pk = P.din("cw_pk", [L, 128, 24, 4])
    nf_in = P.din("nf", [128, 1]); mask_in = P.din("mask", [128, 8]); maskp_in = P.din("maskp", [128, 8])
    idf_in = P.din("ident_f", [128, 128]); idb_in = P.din("ident_b", [128, 128], BF16)
    s5in = {k: P.din("s5_" + k, (v if k in S5_CONST else [L] + v)) for k, v in S5_SHAPES.items()}
    d_pk = P.din("d_pk", [L, 128, 8]); bglu_pk = P.din("bglu_pk", [L, 128, 8]); wglu = P.din("wglu", [L, 1024, 1024])
    gconst = {k: P.din("gc_" + k, [128, 128]) for k in GDN_CONST_NAMES}
    gin = {"a_log": P.din("a_log", [L, 8]), "dt_bias": P.din("dt_bias", [L, 8])}
    normw = P.din("norm_w", [L, 128])
    w_up_ssm = P.din("w_up_ssm", [L, 1024, D]); w_up_gdn = P.din("w_up_gdn", [L, 1024, D]); w_mix_out = P.din("w_mix_out", [L, D, D])
    ln1_g = P.din("ln1_g", [L, D]); ln1_b = P.din("ln1_b", [L, D]); ln2_g = P.din("ln2_g", [L, D]); ln2_b = P.din("ln2_b", [L, D])
    ffn_w_in = P.din("ffn_w_in", [L, D, 2 * FH]); ffn_w_out = P.din("ffn_w_out", [L, FH, D])
    out = nc.dram_tensor("out", [T, D], F32, kind="ExternalOutput").ap()
    sh = ncores > 4
    MODB = [P.dscr(f"MODB{l}", [128, 12288]) for l in range(L)]
    QKV = P.dscr("QKV", [T, 3072]); UT = P.dscr("UT", [1024, 8, N8]); Z = P.dscr("Z", [T, 1024]); BA = P.dscr("BA", [T, 16]); G = P.dscr("G", [T, 4096])
    YT = P.dscr("YT", [1024, 8, N8]); S5B = P.dscr("S5B", [128, 64]); S5G = P.dscr("S5G", [ncores * 128, 64], shared=sh)
    UU = P.dscr("UU", [T, 1024]); KT = P.dscr("KT", [T, 1024])
    ATT = P.dscr("ATT", [NT, 128, 8, 128]); WT = P.dscr("WT", [NT, 128, 8, 128]); QDT = P.dscr("QDT", [NT, 128, 8, 128])
    GB = P.dscr("GB", [1024, 256]); GG = P.dscr("GG", [ncores * 1024, 256], shared=sh)
    YM = P.dscr("YM", [T, D]); XMID = P.dscr("XMID", [T, D]); ACTT = P.dscr("ACTT", [FH // 128, 128, T], BF16); YF = P.dscr("YF", [T, D])
    XL = P.dscr("XL", [T, D]); HB = P.dscr("HB", [3, D]); HG = P.dscr("HG", [ncores * 3, D], shared=sh); XH = P.dscr("XH", [3, D])
    with ExitStack() as st0:
        ident_f = P.sb(st0, "ident_f", [128, 128]); ident_bf = P.sb(st0, "ident_bf", [128, 128], BF16)
        nf = P.sb(st0, "nf", [128, 1]); mask = P.sb(st0, "mask", [128, 8]); maskp = P.sb(st0, "maskp", [128, 8])
        GL = P.sb(st0, "GL", [128, NT, 2, 8])
        for t, src, k in ((ident_f, idf_in, "ident_f"), (ident_bf, idb_in, "ident_bf"), (nf, nf_in, "nf"), (mask, mask_in, "mask"), (maskp, maskp_in, "maskp")):
            fw.dma("sp", t[:], src, w=[k])
        phase_mod(P, c_pk, w_ada, b_ada, MODB)
        _STOP[0] = False
        if True:
          for l in range(L):
              xsrc = x_in if l == 0 else XL
              xhsrc = xh_in if l == 0 else XH
              xdst = out if l == L - 1 else XL
              with ExitStack() as st1:
                  hT = P.sb(st1, "hT", [128, 16, P.HT], BF16)
                  srcs = [(xhsrc, [], 0, 3)] + [(xsrc[tt * 128:(tt + 1) * 128, :], [], 3 + tt * 128, 128) for tt in range(NT)]
                  _ph(phase_ln_mod_T, P, st1, srcs, MODB, l, 1, 0, hT, "hT", ident_bf)
                  _ph(phase_inproj, P, l, hT, w_in, cw_pk, nf, ident_f, QKV, UT, Z, BA, G)
              _ck('inproj')
              with ExitStack() as stw:
                  W = _ph(s5_precompute, P, stw, l, s5in, ident_f)
                  _ph(phase_s5_main, P, l, W, UT, YT, S5B, S5G, mask)
              _ck('s5main')
              with ExitStack() as stm:
                  s5oT = _ph(phase_s5_post, P, l, stm, UT, YT, d_pk, wglu, bglu_pk)
                  _ck('s5post')
                  _ph(phase_gdn_pre, P, l, QKV, BA, gin, gconst, ident_f, GL, UU, KT, ATT, WT, QDT)
                  _ck('gpre')
                  _ph(phase_gdn_pass1, P, l, GL, UU, KT, WT, ident_f, GB, GG)
                  _ck('gp1')
                  S0 = _ph(phase_gdn_chain, P, stm, GG, mask, ident_f)
                  _ck('gchain')
                  gdnT = _ph(phase_gdn_pass2, P, l, stm, S0, GL, UU, KT, WT, ATT, QDT, Z, normw, ident_bf)
                  _ck('gp2')
                  with ExitStack() as stM:
                      MT = _ph(phase_merge, P, l, stM, s5oT, gdnT, w_up_ssm, w_up_gdn, G, ident_bf)
                      _ck('merge')
                      _ph(proj_tokmajor, P, MT, lambda tt: [("MT", tt)], 16, 0, NT, w_mix_out[l], D, 512, YM, 0, "po")
              _ck('proj')
              _ph(phase_resid_ln, P, l, xsrc, YM, MODB, 2, ln1_g, ln1_b, XMID)
              _ck('res1')
              with ExitStack() as st2:
                  h2T = P.sb(st2, "h2T", [128, 16, T], BF16)
                  srcs = [(XMID[tt * 128:(tt + 1) * 128, :], [], tt * 128, 128) for tt in range(NT)]
                  _ph(phase_ln_mod_T, P, st2, srcs, MODB, l, 4, 3, h2T, "h2T", ident_bf)
                  _ph(phase_ffn_in, P, l, h2T, ffn_w_in, ACTT)
              _ck('ffnin')
              _ph(phase_ffn_out, P, l, ACTT, ffn_w_out, YF)
              _ck('ffnout')
              _ph(phase_resid_ln, P, l, XMID, YF, MODB, 5, ln2_g, ln2_b, xdst)
              if l < L - 1:
                  _ph(phase_halo, P, XL, HB, HG, maskp, XH)
        fw.barrier()
    return P


def host_inputs(inp, T, ncores, depth):
    import ml_dtypes
    L = depth
    f = lambda k: np.asarray(inp[k], dtype=np.float32)
    x = f("x")[0]
    shared = {}
    shared["c_pk"] = _c(f("c")[0].reshape(16, 128).T)
    for k in ("w_ada", "b_ada", "w_in", "w_up_ssm", "w_up_gdn", "w_mix_out", "ln1_g", "ln1_b", "ln2_g", "ln2_b", "ffn_w_in", "ffn_w_out"):
        shared[k] = np.ascontiguousarray(f(k)[:L])
    shared["cw_pk"] = _c(f("gdn_conv_w")[:L].reshape(L, 4, 24, 128).transpose(0, 3, 2, 1))
    shared["ident_f"] = np.eye(128, dtype=np.float32)
    shared["ident_b"] = np.eye(128).astype(ml_dtypes.bfloat16)
    H = s5_host(f("ssm_lam_re")[:L], f("ssm_lam_im")[:L], f("ssm_log_dt")[:L], f("ssm_b_re")[:L], f("ssm_b_im")[:L], f("ssm_c_re")[:L], f("ssm_c_im")[:L])
    for k in S5_SHAPES:
        shared["s5_" + k] = H[k]
    shared["d_pk"] = pk8(f("ssm_d")[:L]); shared["bglu_pk"] = pk8(f("ssm_b_glu")[:L]); shared["wglu"] = np.ascontiguousarray(f("ssm_w_glu")[:L])
    for k, v in gdn_consts_host().items():
        shared["gc_" + k] = v
    shared["a_log"] = _c(f("gdn_a_log")[:L]); shared["dt_bias"] = _c(f("gdn_dt_bias")[:L]); shared["norm_w"] = _c(f("gdn_norm_w")[:L])
    maps = []
    for c in range(ncores):
        d = dict(shared)
        d["x"] = np.ascontiguousarray(x[c * T:(c + 1) * T])
        d["xh"] = np.ascontiguousarray(x[c * T - 3:c * T]) if c > 0 else np.zeros((3, D), np.float32)
        d["nf"] = np.full((128, 1), 0.0 if c == 0 else 1.0, np.float32)
        m = np.zeros((128, 8), np.float32); m[:, c] = 1.0; d["mask"] = m
        mp = np.zeros((128, 8), np.float32)
        if c > 0:
            mp[:, c - 1] = 1.0
        d["maskp"] = mp
        maps.append(d)
    return maps


_PROG_CACHE = {}


def run_model(inp, T, ncores, depth):
    key = (T, ncores, depth)
    if key not in _PROG_CACHE:
        _PROG_CACHE[key] = build(T, ncores, depth)
    P = _PROG_CACHE[key]
    maps = host_inputs(inp, T, ncores, depth)
    res = run_bass_kernel_spmd(P.nc, maps, core_ids=list(range(ncores)))
    return np.concatenate([res.results[c]["out"] for c in range(ncores)], axis=0)[None].astype(np.float32)


def kernel(**inputs):
    return run_model(inputs, 16384 // NCORES, NCORES, DEPTH)
```
